# Optimizing a Trainium2 kernel written in Bass

```python
import jax
import jax.numpy as jnp
from jax import lax
import numpy as np

D_MODEL = 1024
BATCH = 16
SEQ = 2048
DEPTH = 1

N_MEM = 256
BLOCK = 128
FOX_HEADS = 8
FOX_HD = 64
SWA_HEADS = 8
SWA_KV_HEADS = 2
SWA_HD = 64
WINDOW = 128
MEM_HEADS = 4
MEM_HD = 128
N_BRANCH = 3
PEER_HEADS = 8
N_KEYS = 128
N_EXPERTS = N_KEYS * N_KEYS
PEER_QD = 256
PEER_TOPK = 16
PEER_CHUNK = 128
EPS = 1e-6
NEG_INF = -1e30

FOX_W = FOX_HEADS * FOX_HD
SWA_W = SWA_HEADS * SWA_HD
SWA_KV_W = SWA_KV_HEADS * SWA_HD
MEM_W = MEM_HEADS * MEM_HD
IN_WIDTHS = (FOX_W, FOX_W, FOX_W, FOX_HEADS, SWA_W, SWA_KV_W, SWA_KV_W, MEM_W, N_BRANCH * D_MODEL)
IN_COLS = 3 * FOX_W + FOX_HEADS + SWA_W + 2 * SWA_KV_W + MEM_W + N_BRANCH * D_MODEL

kernel_name = 'hybrid_fox_swa_mem_peer_block'


def rmsnorm(x, g):
    xf = x.astype(jnp.float32)
    y = xf * lax.rsqrt(jnp.mean(xf * xf, axis=-1, keepdims=True) + EPS)
    return (y * g.astype(jnp.float32)).astype(x.dtype)


def head_rmsnorm(x, g):
    xf = x.astype(jnp.float32)
    return xf * lax.rsqrt(jnp.mean(xf * xf, axis=-1, keepdims=True) + EPS) * g.astype(jnp.float32)


def alibi_slopes(n):
    return 2.0 ** (-8.0 * (jnp.arange(n, dtype=jnp.float32) + 1.0) / n)


def fox_attention(q, k, v, log_f):
    B, S, H, d = q.shape
    nb = S // BLOCK
    c = jnp.cumsum(log_f, axis=1).transpose(0, 2, 1)
    q_blocks = q.reshape(B, nb, BLOCK, H, d).transpose(1, 0, 2, 3, 4)
    c_blocks = c.reshape(B, H, nb, BLOCK).transpose(2, 0, 1, 3)
    vf = v.astype(jnp.float32)
    kpos = jnp.arange(S)
    scale = d ** -0.5

    def one_block(args):
        i, q_i, c_i = args
        s = jnp.einsum('bqhd,bkhd->bhqk', q_i, k) * scale
        s = s + (c_i[..., :, None] - c[..., None, :])
        qpos = i * BLOCK + jnp.arange(BLOCK)
        s = jnp.where(kpos[None, :] <= qpos[:, None], s, NEG_INF)
        p = jax.nn.softmax(s, axis=-1)
        return jnp.einsum('bhqk,bkhd->bqhd', p, vf)

    o = lax.map(one_block, (jnp.arange(nb), q_blocks, c_blocks))
    return o.transpose(1, 0, 2, 3, 4).reshape(B, S, H, d)


def swa_attention(q, k, v, sinks, slopes):
    B, S, Hq, d = q.shape
    G = k.shape[2]
    R = Hq // G
    nb = S // BLOCK
    qb = q.reshape(B, nb, BLOCK, G, R, d)

    def band(t):
        t = t.astype(jnp.float32)
        tp = jnp.concatenate([jnp.zeros((B, BLOCK, G, d), jnp.float32), t], axis=1)
        tp = tp.reshape(B, nb + 1, BLOCK, G, d)
        return jnp.concatenate([tp[:, :-1], tp[:, 1:]], axis=2)

    kb = band(k)
    vb = band(v)
    s = jnp.einsum('bnqgrd,bnkgd->bngrqk', qb, kb) * d ** -0.5
    qi = jnp.arange(BLOCK)[:, None]
    kj = jnp.arange(2 * BLOCK)[None, :]
    dist_i = qi + BLOCK - kj
    in_window = (dist_i >= 0) & (dist_i < WINDOW)
    blk_ok = (jnp.arange(nb)[:, None] > 0) | (jnp.arange(2 * BLOCK)[None, :] >= BLOCK)
    mask = in_window[None] & blk_ok[:, None, :]
    s = s - slopes.astype(jnp.float32).reshape(G, R)[:, :, None, None] * dist_i.astype(jnp.float32)
    s = jnp.where(mask[None, :, None, None], s, NEG_INF)
    sink = jnp.broadcast_to(sinks.astype(jnp.float32).reshape(1, 1, G, R, 1, 1), s.shape[:-1] + (1,))
    p = jax.nn.softmax(jnp.concatenate([s, sink], axis=-1), axis=-1)[..., :-1]
    o = jnp.einsum('bngrqk,bnkgd->bnqgrd', p, vb)
    return o.reshape(B, S, Hq, d)


def memory_attention(q, k, v):
    d = q.shape[-1]
    s = jnp.einsum('bshd,bmhd->bhsm', q, k) * d ** -0.5
    p = jax.nn.softmax(s, axis=-1)
    return jnp.einsum('bhsm,bmhd->bshd', p, v.astype(jnp.float32))


def peer_ffn(h, w_q, keys1, keys2, u, v):
    B, S, D = h.shape
    half = PEER_QD // 2
    q = (h @ w_q).astype(jnp.float32).reshape(B, S, PEER_HEADS, 2, half)
    s1 = jnp.einsum('bshd,hkd->bshk', q[..., 0, :], keys1.astype(jnp.float32))
    s2 = jnp.einsum('bshd,hkd->bshk', q[..., 1, :], keys2.astype(jnp.float32))
    v1, i1 = lax.top_k(s1, PEER_TOPK)
    v2, i2 = lax.top_k(s2, PEER_TOPK)
    cand = (v1[..., :, None] + v2[..., None, :]).reshape(B, S, PEER_HEADS, PEER_TOPK * PEER_TOPK)
    sv, si = lax.top_k(cand, PEER_TOPK)
    e1 = jnp.take_along_axis(i1, si // PEER_TOPK, axis=-1)
    e2 = jnp.take_along_axis(i2, si % PEER_TOPK, axis=-1)
    experts = e1 * N_KEYS + e2
    gates = jax.nn.softmax(sv, axis=-1)
    n_act = PEER_HEADS * PEER_TOPK
    n_chunks = (B * S) // PEER_CHUNK
    hc = h.reshape(n_chunks, PEER_CHUNK, D)
    ec = experts.reshape(n_chunks, PEER_CHUNK, n_act)
    gc = gates.reshape(n_chunks, PEER_CHUNK, n_act)

    def chunk(args):
        h_c, e_c, g_c = args
        a = jnp.einsum('cd,ced->ce', h_c.astype(jnp.float32), u[e_c].astype(jnp.float32))
        act = jax.nn.gelu(a, approximate=False) * g_c
        return jnp.einsum('ce,ced->cd', act, v[e_c].astype(jnp.float32))

    out = lax.map(chunk, (hc, ec, gc))
    return out.reshape(B, S, D).astype(h.dtype)


def setup_inputs(seed: int = 0) -> dict:
    key = jax.random.key(seed)
    ks = jax.random.split(key, 25)
    L = DEPTH
    D = D_MODEL

    def nrm(k, shape, scale):
        return scale * jax.random.normal(k, shape, jnp.float32)

    return {
        'x': nrm(ks[0], (BATCH, SEQ, D), 1.0),
        'mem': nrm(ks[1], (BATCH, N_MEM, D), 1.0),
        'norm1_g': 1.0 + nrm(ks[2], (L, D), 0.02),
        'w_in': nrm(ks[3], (L, D, IN_COLS), D ** -0.5),
        'b_gate': nrm(ks[4], (L, N_BRANCH * D), 0.02),
        'b_forget': 2.0 + nrm(ks[5], (L, FOX_HEADS), 0.5),
        'fox_q_g': 1.0 + nrm(ks[6], (L, FOX_HD), 0.02),
        'fox_k_g': 1.0 + nrm(ks[7], (L, FOX_HD), 0.02),
        'swa_q_g': 1.0 + nrm(ks[8], (L, SWA_HD), 0.02),
        'swa_k_g': 1.0 + nrm(ks[9], (L, SWA_HD), 0.02),
        'swa_sinks': nrm(ks[10], (L, SWA_HEADS), 0.5),
        'mem_norm_g': 1.0 + nrm(ks[11], (L, D), 0.02),
        'w_mem_kv': nrm(ks[12], (L, D, 2 * MEM_W), D ** -0.5),
        'mem_q_g': 1.0 + nrm(ks[13], (L, MEM_HD), 0.02),
        'mem_k_g': 1.0 + nrm(ks[14], (L, MEM_HD), 0.02),
        'w_fox_o': nrm(ks[15], (L, FOX_W, D), FOX_W ** -0.5),
        'w_swa_o': nrm(ks[16], (L, SWA_W, D), SWA_W ** -0.5),
        'w_mem_o': nrm(ks[17], (L, MEM_W, D), MEM_W ** -0.5),
        'w_out': nrm(ks[18], (L, D, D), D ** -0.5),
        'norm2_g': 1.0 + nrm(ks[19], (L, D), 0.02),
        'w_peer_q': nrm(ks[20], (L, D, PEER_HEADS * PEER_QD), D ** -0.5),
        'peer_keys1': nrm(ks[21], (L, PEER_HEADS, N_KEYS, PEER_QD // 2), (PEER_QD // 2) ** -0.5),
        'peer_keys2': nrm(ks[22], (L, PEER_HEADS, N_KEYS, PEER_QD // 2), (PEER_QD // 2) ** -0.5),
        'peer_u': nrm(ks[23], (L, N_EXPERTS, D), D ** -0.5),
        'peer_v': nrm(ks[24], (L, N_EXPERTS, D), (PEER_HEADS * PEER_TOPK) ** -0.5),
    }


def reference(x, mem, norm1_g, w_in, b_gate, b_forget, fox_q_g, fox_k_g, swa_q_g, swa_k_g,
              swa_sinks, mem_norm_g, w_mem_kv, mem_q_g, mem_k_g, w_fox_o, w_swa_o, w_mem_o,
              w_out, norm2_g, w_peer_q, peer_keys1, peer_keys2, peer_u, peer_v):
    B, S, D = x.shape
    M = mem.shape[1]
    splits = [int(c) for c in np.cumsum(IN_WIDTHS)[:-1]]
    slopes = alibi_slopes(SWA_HEADS)
    for l in range(DEPTH):
        h = rmsnorm(x, norm1_g[l])
        z = h @ w_in[l]
        fq, fk, fv, ff, sq, sk, sv, mq, gl = jnp.split(z, splits, axis=-1)

        fq = head_rmsnorm(fq.reshape(B, S, FOX_HEADS, FOX_HD), fox_q_g[l])
        fk = head_rmsnorm(fk.reshape(B, S, FOX_HEADS, FOX_HD), fox_k_g[l])
        log_f = jax.nn.log_sigmoid((ff + b_forget[l]).astype(jnp.float32))
        o_fox = fox_attention(fq, fk, fv.reshape(B, S, FOX_HEADS, FOX_HD), log_f)
        o_fox = o_fox.reshape(B, S, FOX_W).astype(x.dtype)

        sq = head_rmsnorm(sq.reshape(B, S, SWA_HEADS, SWA_HD), swa_q_g[l])
        sk = head_rmsnorm(sk.reshape(B, S, SWA_KV_HEADS, SWA_HD), swa_k_g[l])
        o_swa = swa_attention(sq, sk, sv.reshape(B, S, SWA_KV_HEADS, SWA_HD), swa_sinks[l], slopes)
        o_swa = o_swa.reshape(B, S, SWA_W).astype(x.dtype)

        mn = rmsnorm(mem, mem_norm_g[l])
        mk, mv = jnp.split(mn @ w_mem_kv[l], 2, axis=-1)
        mk = head_rmsnorm(mk.reshape(B, M, MEM_HEADS, MEM_HD), mem_k_g[l])
        mq = head_rmsnorm(mq.reshape(B, S, MEM_HEADS, MEM_HD), mem_q_g[l])
        o_mem = memory_attention(mq, mk, mv.reshape(B, M, MEM_HEADS, MEM_HD))
        o_mem = o_mem.reshape(B, S, MEM_W).astype(x.dtype)

        gates = jax.nn.sigmoid((gl + b_gate[l]).astype(jnp.float32)).astype(x.dtype)
        gates = gates.reshape(B, S, N_BRANCH, D)
        merged = (gates[:, :, 0] * (o_fox @ w_fox_o[l])
                  + gates[:, :, 1] * (o_swa @ w_swa_o[l])
                  + gates[:, :, 2] * (o_mem @ w_mem_o[l]))
        x = x + merged @ w_out[l]

        x = x + peer_ffn(rmsnorm(x, norm2_g[l]), w_peer_q[l], peer_keys1[l], peer_keys2[l],
                         peer_u[l], peer_v[l])
    return x
```

```python
import contextlib
import numpy as np
import ml_dtypes
import concourse.bass as bass
import concourse.mybir as mybir
from concourse.bass_utils import run_bass_kernel_spmd

F32 = mybir.dt.float32
BF16 = mybir.dt.bfloat16
I32 = mybir.dt.int32
U32 = mybir.dt.uint32
ALU = mybir.AluOpType
AF = mybir.ActivationFunctionType
AX = mybir.AxisListType

P = 128
D = 1024
NCORES = 8
SEQ = 2048
NSEQ = 2
NBLK = SEQ // P
NGRP = 4
GT = 4
GN = GT * P
NMEM = 256
EPS = 1e-6
NEXP = 16384
RING = 8


class Buf:
    __slots__ = ("name", "w", "r", "excl")

    def __init__(self, name, excl=False):
        self.name = name
        self.w = None
        self.r = {}
        self.excl = excl


class Sched:
    ENGS = ("pe", "act", "dve", "pool", "sp")

    def __init__(self, nc, stack):
        self.nc = nc
        self.stack = stack
        self.eng = {"pe": nc.tensor, "act": nc.scalar, "dve": nc.vector,
                    "pool": nc.gpsimd, "sp": nc.sync}
        self.sem = {}
        self.cnt = {}
        self.waited = {e: {} for e in self.ENGS}
        self.owner = {}
        self.stopped = [False]
        for e in self.ENGS:
            self.sem[e] = stack.enter_context(nc.semaphore("s_" + e))
            self.cnt[e] = 0

    def _wait(self, e, key, val):
        if key in self.ENGS:
            if key == e:
                if e in ("pe", "sp"):
                    return
                if val < self.cnt[e] - 1:
                    return
        else:
            val = self.cnt[key]
        if self.waited[e].get(key, 0) >= val:
            return
        self.waited[e][key] = val
        self.eng[e].wait_ge(self.sem[key], val)

    def _deps(self, e, reads, writes):
        deps = {}
        for b in reads:
            if b.w is not None:
                k, v = b.w
                if deps.get(k, 0) < v:
                    deps[k] = v
        for b in writes:
            if b.w is not None:
                k, v = b.w
                if deps.get(k, 0) < v:
                    deps[k] = v
            for k, v in b.r.items():
                if deps.get(k, 0) < v:
                    deps[k] = v
        for k, v in deps.items():
            self._wait(e, k, v)

    def _mark(self, ticket, reads, writes):
        k, v = ticket
        for b in reads:
            if b.r.get(k, 0) < v:
                b.r[k] = v
        for b in writes:
            b.w = ticket
            b.r = {}

    def op(self, e, fn, reads=(), writes=(), sig=True):
        if self.stopped[0]:
            return None
        if any(b.excl for b in reads):
            writes = list(writes) + [b for b in reads if b.excl]
            reads = [b for b in reads if not b.excl]
        self._deps(e, reads, writes)
        inst = fn(self.eng[e])
        if sig:
            inst.then_inc(self.sem[e], 1)
            self.cnt[e] += 1
            ticket = (e, self.cnt[e])
        else:
            ticket = (e, self.cnt[e] + 1)
        self._mark(ticket, reads, writes)
        return inst

    def dma(self, q, semkey, fn, reads=(), writes=()):
        if self.stopped[0]:
            return None
        if semkey not in self.sem:
            self.sem[semkey] = self.stack.enter_context(self.nc.semaphore("s_" + semkey))
            self.cnt[semkey] = 0
            self.owner[semkey] = q
        assert self.owner[semkey] == q
        self._deps(q, reads, writes)
        inst = fn(self.eng[q])
        inst.then_inc(self.sem[semkey], 16)
        self.cnt[semkey] += 16
        self._mark((semkey, self.cnt[semkey]), reads, writes)
        return inst

    def barrier(self):
        if self.stopped[0]:
            return
        for e in self.ENGS:
            for k in list(self.sem):
                if k != e and self.cnt[k] > 0:
                    v = self.cnt[k]
                    if self.waited[e].get(k, 0) < v:
                        self.waited[e][k] = v
                        self.eng[e].wait_ge(self.sem[k], v)

    def finish(self):
        for k in list(self.sem):
            if k != "sp" and self.cnt[k] > 0:
                v = self.cnt[k]
                if self.waited["sp"].get(k, 0) < v:
                    self.waited["sp"][k] = v
                    self.eng["sp"].wait_ge(self.sem[k], v)


class _StopBuild(Exception):
    pass


def build_program(nseq=NSEQ, ngrp=NGRP, npeer=None, phase_a=True, stop_at=None, nexp=NEXP):
    _stopped = [False]
    def stage(n):
        if stop_at is not None and n >= stop_at:
            _stopped[0] = True

    nc = bass.Bass("TRN2", target_bir_lowering=False)

    def din(name, shape, dt=F32):
        return nc.dram_tensor(name, list(shape), dt, kind="ExternalInput").ap()

    x_d = din("x", [NSEQ * SEQ, D])
    mem_d = din("mem", [NSEQ * NMEM, D])
    out_d = nc.dram_tensor("out", [NSEQ * SEQ, D], F32, kind="ExternalOutput").ap()
    wa1_d = din("wa1", [P, 6, 8, 512])
    wmkv_d = din("wmkv", [P, 2, 8, 512])
    wff_d = din("wff", [P, 8, 8])
    wa3_d = din("wa3", [P, 8, 4608])
    wout_d = din("wout", [P, 8, D])
    wpq_d = din("wpq", [P, 8, 2048])
    keys_d = din("keysT", [P, 16, P])
    u_d = din("peer_u", [nexp, D])
    v_d = din("peer_v", [nexp, D])
    g1_d = din("g1", [1, D])
    g2_d = din("g2", [1, D])
    gm_d = din("gm", [1, D])
    bf_d = din("bforget", [1, 8])
    sink_d = din("sinks", [1, 8])
    gcols_d = din("gcols", [P, 18])
    bgate_d = din("bgate", [P, 24])
    identf_d = din("identf", [P, P])
    identb_d = din("identb", [P, P], BF16)
    onesblk_d = din("onesblk", [P, P], BF16)
    onesfull_d = din("onesfull", [P, P], BF16)
    tri_d = din("tri", [P, P])
    onesf_d = din("onesf", [P, P])
    cmask_d = din("cmask", [P, P], BF16)
    eb_d = din("eb", [P, 8, 2, P])
    iota_d = din("iota16", [P, 16])

    with contextlib.ExitStack() as st:
        S = Sched(nc, st)
        S.stopped = _stopped

        def sb(stack, name, shape, dt=F32):
            return stack.enter_context(nc.sbuf_tensor(name, list(shape), dt))

        banks = [st.enter_context(nc.psum_tensor("pb%d" % i, [P, 512], F32)) for i in range(8)]
        bbufs = [Buf("bank%d" % i, excl=True) for i in range(8)]
        pstate = {"L": 0, "S": 0}

        def palloc(n):
            if n == 4:
                b = pstate["L"]
                pstate["L"] = (b + 1) % 4
            else:
                b = 4 + pstate["S"]
                pstate["S"] = (pstate["S"] + 1) % 4
            return banks[b][:, 0:n * 128], [bbufs[b]]

        def cload(name, src, shape, dt=F32, bc=False):
            t = sb(st, name, shape, dt)
            b = Buf(name)
            S.dma("sp", "dconst", lambda e: e.dma_start(out=t[:], in_=(src.partition_broadcast(P) if bc else src)), writes=[b])
            return t, b

        identf, b_identf = cload("identf_s", identf_d, [P, P])
        identb, b_identb = cload("identb_s", identb_d, [P, P], BF16)
        g1bc, b_g1 = cload("g1_s", g1_d, [P, D], bc=True)
        iota16, b_iota = cload("iota_s", iota_d, [P, 16])
        CB = [b_identf, b_identb, b_g1, b_iota]

        def rmsnorm(eng_stack, xt, bx, gbc, bg, ht, bh, ss, bss):
            S.op("dve", lambda e: e.memset(ss[:], 0.0), writes=[bss])
            S.op("act", lambda e: e.activation(out=ht, in_=xt, func=AF.Square, accum_out=ss[:]),
                 reads=[bx], writes=[bh, bss])
            S.op("act", lambda e: e.activation(out=ss[:], in_=ss[:], func=AF.Sqrt, bias=EPS, scale=1.0 / D),
                 reads=[bss], writes=[bss])
            S.op("dve", lambda e: e.reciprocal(out=ss[:], in_=ss[:]), reads=[bss], writes=[bss])
            S.op("dve", lambda e: e.scalar_tensor_tensor(out=ht, in0=xt, scalar=ss[:, 0:1], in1=gbc[:],
                                                          op0=ALU.mult, op1=ALU.mult),
                 reads=[bx, bss, bg], writes=[bh])

        def transpose8(ht, bh):
            regs = []
            for hb in range(2):
                pr, pbs = palloc(4)
                for c4 in range(4):
                    c = hb * 4 + c4
                    S.op("pe", lambda e, c=c, c4=c4, pr=pr: e.transpose(out=pr[:, c4 * 128:(c4 + 1) * 128],
                                                                      in_=ht[:, c * 128:(c + 1) * 128], identity=identf[:]),
                         reads=[bh, b_identf], writes=pbs, sig=(c4 == 3))
                regs.append((pr, pbs))
            return regs

        try:
          with contextlib.ExitStack() as sa:
            onesblk, b_onesblk = None, None
            t_onesblk = sb(sa, "onesblk_s", [P, P], BF16); b_onesblk = Buf("onesblk")
            t_onesfull = sb(sa, "onesfull_s", [P, P], BF16); b_onesfull = Buf("onesfull")
            t_tri = sb(sa, "tri_s", [P, P]); b_tri = Buf("tri")
            t_onesf = sb(sa, "onesf_s", [P, P]); b_onesf = Buf("onesf")
            t_cmask = sb(sa, "cmask_s", [P, P], BF16); b_cmask = Buf("cmask")
            t_eb = sb(sa, "eb_s", [P, 8, 2, P]); b_eb = Buf("eb")
            t_gm = sb(sa, "gm_s", [P, D]); b_gm = Buf("gm")
            t_bf = sb(sa, "bf_s", [P, 8]); b_bf = Buf("bf")
            t_sink = sb(sa, "sink_s", [P, 8]); b_sink = Buf("sink")
            t_gcols = sb(sa, "gcols_s", [P, 18]); b_gcols = Buf("gcols")
            t_bgate = sb(sa, "bgate_s", [P, 24]); b_bgate = Buf("bgate")
            t_wff = sb(sa, "wff_s", [P, 8, 8]); b_wff = Buf("wff")
            t_wout = sb(sa, "wout_s", [P, 8, D], BF16); b_wout = Buf("wout")
            for t, b, src, bc in ((t_onesblk, b_onesblk, onesblk_d, False), (t_onesfull, b_onesfull, onesfull_d, False),
                                  (t_tri, b_tri, tri_d, False), (t_onesf, b_onesf, onesf_d, False),
                                  (t_cmask, b_cmask, cmask_d, False), (t_eb, b_eb, eb_d, False),
                                  (t_gm, b_gm, gm_d, True), (t_bf, b_bf, bf_d, True), (t_sink, b_sink, sink_d, True),
                                  (t_gcols, b_gcols, gcols_d, False), (t_bgate, b_bgate, bgate_d, False),
                                  (t_wff, b_wff, wff_d, False)):
                S.dma("sp", "dconst", lambda e, t=t, src=src, bc=bc: e.dma_start(
                    out=t[:], in_=(src.partition_broadcast(P) if bc else src)), writes=[b])
            S.dma("pool", "dwout", lambda e: e.dma_start(out=t_wout[:], in_=wout_d), writes=[b_wout])
            S.op("act", lambda e: e.activation(out=t_sink[:], in_=t_sink[:], func=AF.Exp), reads=[b_sink], writes=[b_sink])

            stage(1)
            NW = 3
            wring = [sb(sa, "wring%d" % i, [P, 4608], BF16) for i in range(NW)]
            b_wring = [Buf("wring%d" % i) for i in range(NW)]
            wstate = {"i": 0}

            def wload(src_ap, ncols):
                i = wstate["i"]
                wstate["i"] = (i + 1) % NW
                ov = wring[i][:, 0:ncols]
                if len(src_ap.shape) == 3:
                    ov = ov.rearrange("p (c n) -> p c n", n=src_ap.shape[2])
                S.dma("pool", "dw%d" % i, lambda e: e.dma_start(out=ov, in_=src_ap), writes=[b_wring[i]])
                return wring[i], b_wring[i]

            kF = sb(sa, "kF", [P, 4, SEQ], BF16); b_kF = [Buf("kF%d" % i) for i in range(NGRP)]
            vF = sb(sa, "vF", [P, NBLK, 8, 72], BF16); b_vF = [Buf("vF%d" % i) for i in range(NBLK)]
            kS = sb(sa, "kS", [P, SEQ], BF16); b_kS = [Buf("kS%d" % i) for i in range(NGRP)]
            vS = sb(sa, "vS", [P, NBLK, 2, 72], BF16); b_vS = [Buf("vS%d" % i) for i in range(NBLK)]
            kM = sb(sa, "kM", [P, 4, NMEM], BF16); b_kM = Buf("kM")
            vM = sb(sa, "vM", [P, 2, 4, 136], BF16); b_vM = Buf("vM")
            S.op("dve", lambda e: e.memset(vF[:], 1.0), writes=b_vF)
            S.op("dve", lambda e: e.memset(vS[:], 1.0), writes=b_vS)
            S.op("dve", lambda e: e.memset(vM[:], 1.0), writes=[b_vM])
            Cc = sb(sa, "Cc", [P, 8, NBLK]); b_Cc = Buf("Cc")
            Off = sb(sa, "Off", [P, 8, NBLK]); b_Off = Buf("Off")
            carry = [sb(sa, "carry%d" % i, [P, 8]) for i in range(2)]
            b_carry = [Buf("carry0"), Buf("carry1")]
            xg = sb(sa, "xg", [P, 2, D]); b_xg = [Buf("xg%d" % i) for i in range(2)]
            hT = sb(sa, "hT", [P, 8, GN], BF16); b_hT = [Buf("hT%d" % i) for i in range(GT)]
            hT32 = sb(sa, "hT32", [P, 8, P]); b_hT32 = Buf("hT32")
            h_t = sb(sa, "h_t", [P, D]); b_h = Buf("h")
            ss = sb(sa, "ssA", [P, 1]); b_ss = Buf("ssA")
            qF = sb(sa, "qF", [P, 4, GN], BF16); b_qF = Buf("qF")
            qS = sb(sa, "qS", [P, 4, GN], BF16); b_qS = Buf("qS")
            qM = sb(sa, "qM", [P, 4, GN], BF16); b_qM = Buf("qM")
            ffs = sb(sa, "ffs", [P, GT, 8]); b_ffs = Buf("ffs")
            spl = sb(sa, "spl", [P, GT, 8]); b_spl = Buf("spl")
            sq = [sb(sa, "sq%d" % i, [P, GN], BF16) for i in range(2)]; b_sq = [Buf("sq0"), Buf("sq1")]
            rs = [sb(sa, "rs%d" % i, [P, GN]) for i in range(2)]; b_rs = [Buf("rs0"), Buf("rs1")]
            NPT = 6
            pT = [sb(sa, "pT%d" % i, [P, 2, P], BF16) for i in range(NPT)]; b_pT = [Buf("pT%d" % i) for i in range(NPT)]
            ptstate = {"i": 0}
            esw = [sb(sa, "esw%d" % i, [P, 2, P]) for i in range(2)]; b_esw = [Buf("esw0"), Buf("esw1")]
            biasF = [sb(sa, "biasF%d" % i, [P, NBLK]) for i in range(2)]; b_biasF = [Buf("biasF0"), Buf("biasF1")]
            rec = [sb(sa, "rec%d" % i, [P, 4]) for i in range(2)]; b_rec = [Buf("rec0"), Buf("rec1")]
            otok = [sb(sa, "otok%d" % i, [P, 512], BF16) for i in range(2)]; b_otok = [Buf("otok0"), Buf("otok1")]
            oT = sb(sa, "oT", [P, 12, GN], BF16); b_oT = [[Buf("oT%d_%d" % (br, i)) for i in range(GT)] for br in range(3)]
            gsb = [sb(sa, "gsb%d" % i, [P, GN], BF16) for i in range(3)]; b_gsb = [Buf("gsb%d" % i) for i in range(3)]
            tmg = [sb(sa, "tmg%d" % i, [P, GN]) for i in range(3)]; b_tmg = [Buf("tmg%d" % i) for i in range(3)]
            mT = sb(sa, "mT", [P, 8, GN], BF16); b_mT = [Buf("mT%d" % i) for i in range(8)]
            x1t = [sb(sa, "x1t%d" % i, [P, D]) for i in range(2)]; b_x1t = [Buf("x1t0"), Buf("x1t1")]
            cnts = {"sq": 0, "esw": 0, "bias": 0, "rec": 0, "otok": 0, "x1": 0}

            def nxt(k, n=2):
                v = cnts[k]
                cnts[k] = (v + 1) % n
                return v

            def proj_fm(wsl, bw, s4, rhs, brhs, N, dest, bdest, ones_t, b_ones, hd, gcol):
                pz, pzb = palloc(4)
                for c in range(8):
                    S.op("pe", lambda e, c=c: e.matmul(pz[:, 0:N], lhsT=wsl[:, c * 512 + s4 * 128: c * 512 + (s4 + 1) * 128],
                                                        rhs=rhs[:, c, :], start=(c == 0), stop=(c == 7)),
                         reads=[bw] + brhs, writes=pzb, sig=(c == 7))
                k = nxt("sq")
                S.op("act", lambda e: e.activation(out=sq[k][:, 0:N], in_=pz[:, 0:N], func=AF.Square), reads=pzb, writes=[b_sq[k]])
                pss, pssb = palloc(4)
                S.op("pe", lambda e: e.matmul(pss[:, 0:N], lhsT=ones_t[:], rhs=sq[k][:, 0:N], start=True, stop=True),
                     reads=[b_sq[k], b_ones], writes=pssb)
                S.op("act", lambda e: e.activation(out=rs[k][:, 0:N], in_=pss[:, 0:N], func=AF.Sqrt, bias=EPS, scale=1.0 / hd),
                     reads=pssb, writes=[b_rs[k]])
                S.op("dve", lambda e: e.reciprocal(out=rs[k][:, 0:N], in_=rs[k][:, 0:N]), reads=[b_rs[k]], writes=[b_rs[k]])
                S.op("dve", lambda e: e.scalar_tensor_tensor(out=dest, in0=pz[:, 0:N], scalar=gcol, in1=rs[k][:, 0:N],
                                                              op0=ALU.mult, op1=ALU.mult),
                     reads=pzb + [b_rs[k], b_gcols], writes=bdest)

            for sq_i in range(nseq if phase_a else 0):
                tok0 = sq_i * SEQ
                mnT = hT
                for mb in range(2):
                    r0 = sq_i * NMEM + mb * P
                    S.dma("sp", "dx0", lambda e, r0=r0: e.dma_start(out=xg[:, 0, :], in_=mem_d[r0:r0 + P, :]), writes=[b_xg[0]])
                    rmsnorm(sa, xg[:, 0, :], b_xg[0], t_gm, b_gm, h_t[:], b_h, ss, b_ss)
                    regs = transpose8(h_t, b_h)
                    for hb, (pr, pbs) in enumerate(regs):
                        S.op("act", lambda e, hb=hb, pr=pr, mb=mb: e.copy(
                            out=mnT[:, hb * 4:(hb + 1) * 4, mb * P:(mb + 1) * P],
                            in_=pr.rearrange("p (c n) -> p c n", n=P)), reads=pbs, writes=[b_hT[mb]])
                stage(2)
                wk, bwk = wload(wmkv_d[:, 0, :, :], 4096)
                stage(2.3)
                for hd_i in range(4):
                    proj_fm(wk, bwk, hd_i, mnT[:, :, 0:NMEM], [b_hT[0], b_hT[1]], NMEM, kM[:, hd_i, :], [b_kM],
                            t_onesfull, b_onesfull, 128.0, t_gcols[:, 17:18])
                stage(2.6)
                wv, bwv = wload(wmkv_d[:, 1, :, :], 4096)
                for mb in range(2):
                    pv, pvb = palloc(4)
                    for c in range(8):
                        S.op("pe", lambda e, c=c, mb=mb: e.matmul(pv, lhsT=mnT[:, c, mb * P:(mb + 1) * P], rhs=wv[:, c * 512:(c + 1) * 512],
                                                                 start=(c == 0), stop=(c == 7)),
                             reads=[bwv, b_hT[mb]], writes=pvb, sig=(c == 7))
                    S.op("act", lambda e, mb=mb, pv=pv: e.copy(out=vM[:, mb, :, 0:128], in_=pv.rearrange("p (h d) -> p h d", d=128)),
                         reads=pvb, writes=[b_vM])
                stage(3)
                S.op("dve", lambda e: e.memset(carry[0][:], 0.0), writes=[b_carry[0]])
                cstate = 0

                for g in range(ngrp):
                    for tl in range(GT):
                        blk = g * GT + tl
                        r0 = tok0 + blk * P
                        xb_ = tl % 2
                        S.dma("sp", "dx%d" % xb_, lambda e, r0=r0, xb_=xb_: e.dma_start(out=xg[:, xb_, :], in_=x_d[r0:r0 + P, :]), writes=[b_xg[xb_]])
                        rmsnorm(sa, xg[:, xb_, :], b_xg[xb_], g1bc, b_g1, h_t[:], b_h, ss, b_ss)
                        regs = transpose8(h_t, b_h)
                        for hb, (pr, pbs) in enumerate(regs):
                            S.op("act", lambda e, hb=hb, pr=pr, tl=tl: e.copy(
                                out=hT[:, hb * 4:(hb + 1) * 4, tl * P:(tl + 1) * P],
                                in_=pr.rearrange("p (c n) -> p c n", n=P)), reads=pbs, writes=[b_hT[tl]])
                            S.op("act", lambda e, hb=hb, pr=pr: e.copy(
                                out=hT32[:, hb * 4:(hb + 1) * 4, :], in_=pr.rearrange("p (c n) -> p c n", n=P)),
                                reads=pbs, writes=[b_hT32])
                        stage(3.2)
                        pf, pfb = palloc(1)
                        for c in range(8):
                            S.op("pe", lambda e, c=c: e.matmul(pf[:, 0:8], lhsT=hT32[:, c, :], rhs=t_wff[:, c, :], start=(c == 0), stop=(c == 7)),
                                 reads=[b_hT32, b_wff], writes=pfb, sig=(c == 7))
                        S.op("dve", lambda e, tl=tl, pf=pf: e.tensor_tensor(out=ffs[:, tl, :], in0=pf[:, 0:8], in1=t_bf[:], op=ALU.add),
                             reads=pfb + [b_bf], writes=[b_ffs])
                        stage(3.4)
                    stage(4)
                    S.op("act", lambda e: e.activation(out=spl[:], in_=ffs[:], func=AF.Exp, scale=-1.0), reads=[b_ffs], writes=[b_spl])
                    S.op("act", lambda e: e.activation(out=spl[:], in_=spl[:], func=AF.Ln, bias=1.0, scale=1.0), reads=[b_spl], writes=[b_spl])
                    pc, pcb = palloc(1)
                    S.op("pe", lambda e: e.matmul(pc[:, 0:32], lhsT=t_tri[:], rhs=spl[:].rearrange("p a b -> p (a b)"), start=True, stop=True),
                         reads=[b_tri, b_spl], writes=pcb)
                    S.op("pe", lambda e: e.matmul(pc[:, 32:64], lhsT=t_onesf[:], rhs=spl[:].rearrange("p a b -> p (a b)"), start=True, stop=True),
                         reads=[b_onesf, b_spl], writes=pcb)
                    for tl in range(GT):
                        blk = g * GT + tl
                        c0, c1 = carry[cstate], carry[1 - cstate]
                        bc0, bc1 = b_carry[cstate], b_carry[1 - cstate]
                        S.op("dve", lambda e, blk=blk, c0=c0: e.tensor_copy(out=Off[:, :, blk], in_=c0[:]), reads=[bc0], writes=[b_Off])
                        S.op("dve", lambda e, blk=blk, c0=c0, tl=tl: e.tensor_tensor(out=Cc[:, :, blk], in0=pc[:, tl * 8:(tl + 1) * 8], in1=c0[:], op=ALU.add),
                             reads=pcb + [bc0], writes=[b_Cc])
                        S.op("dve", lambda e, c0=c0, c1=c1, tl=tl: e.tensor_tensor(out=c1[:], in0=pc[:, 32 + tl * 8:32 + (tl + 1) * 8], in1=c0[:], op=ALU.add),
                             reads=pcb + [bc0], writes=[bc1])
                        cstate = 1 - cstate
                    stage(5)
                    gs = slice(g * GN, (g + 1) * GN)
                    w0, bw0 = wload(wa1_d[:, 0, :, :], 4096)
                    for c4 in range(4):
                        proj_fm(w0, bw0, c4, hT, b_hT, GN, qF[:, c4, :], [b_qF], t_onesblk, b_onesblk, 64.0, t_gcols[:, c4:c4 + 1])
                    w1, bw1 = wload(wa1_d[:, 1, :, :], 4096)
                    for c4 in range(4):
                        proj_fm(w1, bw1, c4, hT, b_hT, GN, kF[:, c4, gs], [b_kF[g]], t_onesblk, b_onesblk, 64.0, t_gcols[:, 4 + c4:5 + c4])
                    w2, bw2 = wload(wa1_d[:, 2, :, :], 4096)
                    for c4 in range(4):
                        proj_fm(w2, bw2, c4, hT, b_hT, GN, qS[:, c4, :], [b_qS], t_onesblk, b_onesblk, 64.0, t_gcols[:, 8 + c4:9 + c4])
                    w3, bw3 = wload(wa1_d[:, 3, :, :], 4096)
                    for c4 in range(4):
                        proj_fm(w3, bw3, c4, hT, b_hT, GN, qM[:, c4, :], [b_qM], t_onesfull, b_onesfull, 128.0, t_gcols[:, 12 + c4:13 + c4])
                    w4, bw4 = wload(wa1_d[:, 4, :, :], 4096)
                    proj_fm(w4, bw4, 0, hT, b_hT, GN, kS[:, gs], [b_kS[g]], t_onesblk, b_onesblk, 64.0, t_gcols[:, 16:17])
                    for tl in range(GT):
                        blk = g * GT + tl
                        pv, pvb = palloc(1)
                        for c in range(8):
                            S.op("pe", lambda e, c=c, tl=tl, pv=pv: e.matmul(pv, lhsT=hT[:, c, tl * P:(tl + 1) * P], rhs=w4[:, c * 512 + 128:c * 512 + 256],
                                                                          start=(c == 0), stop=(c == 7)),
                                 reads=[bw4, b_hT[tl]], writes=pvb, sig=(c == 7))
                        S.op("act", lambda e, blk=blk, pv=pv: e.copy(out=vS[:, blk, :, 0:64], in_=pv.rearrange("p (h d) -> p h d", d=64)),
                             reads=pvb, writes=[b_vS[blk]])
                    w5, bw5 = wload(wa1_d[:, 5, :, :], 4096)
                    for tl in range(GT):
                        blk = g * GT + tl
                        pv, pvb = palloc(4)
                        for c in range(8):
                            S.op("pe", lambda e, c=c, tl=tl, pv=pv: e.matmul(pv, lhsT=hT[:, c, tl * P:(tl + 1) * P], rhs=w5[:, c * 512:(c + 1) * 512],
                                                                          start=(c == 0), stop=(c == 7)),
                                 reads=[bw5, b_hT[tl]], writes=pvb, sig=(c == 7))
                        S.op("act", lambda e, blk=blk, pv=pv: e.copy(out=vF[:, blk, :, 0:64], in_=pv.rearrange("p (h d) -> p h d", d=64)),
                             reads=pvb, writes=[b_vF[blk]])

                    stage(6)
                    def finish_branch(br, tl, accs, D_h, nh_per, sink):
                        k = nxt("otok")
                        for ai, (acc, accb) in enumerate(accs):
                            av = acc[:, 0:nh_per * (D_h + 8)].rearrange("p (h d) -> p h d", d=D_h + 8)
                            r = nxt("rec")
                            if sink:
                                S.op("dve", lambda e, av=av, r=r, ai=ai: e.tensor_tensor(
                                    out=rec[r][:, 0:nh_per], in0=av[:, :, D_h], in1=t_sink[:, ai * nh_per:(ai + 1) * nh_per], op=ALU.add),
                                    reads=accb + [b_sink], writes=[b_rec[r]])
                                S.op("dve", lambda e, r=r: e.reciprocal(out=rec[r][:, 0:nh_per], in_=rec[r][:, 0:nh_per]),
                                     reads=[b_rec[r]], writes=[b_rec[r]])
                            else:
                                S.op("dve", lambda e, av=av, r=r: e.reciprocal(out=rec[r][:, 0:nh_per], in_=av[:, :, D_h]),
                                     reads=accb, writes=[b_rec[r]])
                            w = nh_per * D_h
                            S.op("dve", lambda e, av=av, r=r, ai=ai, w=w: e.tensor_tensor(
                                out=otok[k][:, ai * w:(ai + 1) * w].rearrange("p (h d) -> p h d", d=D_h),
                                in0=av[:, :, 0:D_h], in1=rec[r][:, 0:nh_per].unsqueeze(2).to_broadcast([P, nh_per, D_h]), op=ALU.mult),
                                reads=accb + [b_rec[r]], writes=[b_otok[k]])
                        pt_, ptb = palloc(2)
                        ptv = pt_.bitcast(BF16)
                        for c in range(4):
                            S.op("pe", lambda e, c=c: e.transpose(out=ptv[:, c * P:(c + 1) * P], in_=otok[k][:, c * P:(c + 1) * P], identity=identb[:]),
                                 reads=[b_otok[k], b_identb], writes=ptb, sig=(c == 3))
                        S.op("act", lambda e: e.copy(out=oT[:, br * 4:(br + 1) * 4, tl * P:(tl + 1) * P],
                                                     in_=ptv.rearrange("p (c n) -> p c n", n=P)), reads=ptb, writes=[b_oT[br][tl]])

                    for tl in range(GT):
                        i = g * GT + tl
                        qc = slice(tl * P, (tl + 1) * P)
                        accs = []
                        for hg in range(2):
                            acc, accb = palloc(4)
                            accs.append((acc, accb))
                            for hh in range(4):
                                h = hg * 4 + hh
                                pr_, hf = divmod(h, 2)
                                ps_ = slice(hf * 64, hf * 64 + 64)
                                bi = nxt("bias")
                                S.op("dve", lambda e, h=h, bi=bi: e.tensor_scalar(
                                    out=biasF[bi][:, 0:i + 1], in0=Cc[:, h, 0:i + 1], scalar1=Off[:, h, i:i + 1], scalar2=None, op0=ALU.subtract),
                                    reads=[b_Cc, b_Off], writes=[b_biasF[bi]])
                                for j in range(i + 1):
                                    psq, psb = palloc(1)
                                    S.op("pe", lambda e, j=j, psq=psq: e.matmul(psq, lhsT=kF[ps_, pr_, j * P:(j + 1) * P], rhs=qF[ps_, pr_, qc],
                                                                               start=True, stop=True),
                                         reads=[b_kF[j // GT], b_qF], writes=psb)
                                    pi = ptstate["i"]; ptstate["i"] = (pi + 1) % NPT
                                    S.op("act", lambda e, j=j, psq=psq, pi=pi, bi=bi: e.activation(
                                        out=pT[pi][:, 0, :], in_=psq, func=AF.Exp, bias=biasF[bi][:, j:j + 1], scale=0.125),
                                        reads=psb + [b_biasF[bi]], writes=[b_pT[pi]])
                                    if j == i:
                                        S.op("dve", lambda e, pi=pi: e.tensor_tensor(out=pT[pi][:, 0, :], in0=pT[pi][:, 0, :], in1=t_cmask[:], op=ALU.mult),
                                             reads=[b_pT[pi], b_cmask], writes=[b_pT[pi]])
                                    S.op("pe", lambda e, j=j, pi=pi, hh=hh, h=h, acc=acc: e.matmul(
                                        acc[:, hh * 72:hh * 72 + 66], lhsT=pT[pi][:, 0, :], rhs=vF[:, j, h, 0:66], start=(j == 0), stop=(j == i)),
                                        reads=[b_pT[pi], b_vF[j]], writes=accb, sig=(j == i))
                        stage(7)
                        finish_branch(0, tl, accs, 64, 4, False)
                        stage(8)
                        accs = []
                        for hg in range(2):
                            acc, accb = palloc(4)
                            accs.append((acc, accb))
                            for hh in range(4):
                                h = hg * 4 + hh
                                kv = hg
                                ps_ = slice(kv * 64, kv * 64 + 64)
                                qch = hh
                                psq, psb = palloc(2)
                                nb = 2 if i > 0 else 1
                                for w_ in range(nb):
                                    j = i - (nb - 1) + w_
                                    S.op("pe", lambda e, j=j, w_=w_, psq=psq: e.matmul(
                                        psq[:, w_ * P:(w_ + 1) * P], lhsT=kS[ps_, j * P:(j + 1) * P], rhs=qS[ps_, qch, qc], start=True, stop=True),
                                        reads=[b_kS[j // GT], b_qS], writes=psb, sig=(w_ == nb - 1))
                                ei = nxt("esw")
                                S.op("act", lambda e, psq=psq, ei=ei, nb=nb: e.activation(
                                    out=esw[ei][:, 0:nb, :], in_=psq[:, 0:nb * P].rearrange("p (a n) -> p a n", n=P), func=AF.Exp, scale=0.125),
                                    reads=psb, writes=[b_esw[ei]])
                                pi = ptstate["i"]; ptstate["i"] = (pi + 1) % NPT
                                S.op("dve", lambda e, ei=ei, pi=pi, nb=nb, h=h: e.tensor_tensor(
                                    out=pT[pi][:, 0:nb, :], in0=esw[ei][:, 0:nb, :], in1=t_eb[:, h, 2 - nb:2, :], op=ALU.mult),
                                    reads=[b_esw[ei], b_eb], writes=[b_pT[pi]])
                                for w_ in range(nb):
                                    j = i - (nb - 1) + w_
                                    S.op("pe", lambda e, j=j, w_=w_, pi=pi, hh=hh, acc=acc: e.matmul(
                                        acc[:, hh * 72:hh * 72 + 66], lhsT=pT[pi][:, w_, :], rhs=vS[:, j, kv, 0:66], start=(w_ == 0), stop=(w_ == nb - 1)),
                                        reads=[b_pT[pi], b_vS[j]], writes=accb, sig=(w_ == nb - 1))
                        finish_branch(1, tl, accs, 64, 4, True)
                        stage(9)
                        accs = []
                        for hg in range(2):
                            acc, accb = palloc(4)
                            accs.append((acc, accb))
                            for hh in range(2):
                                h = hg * 2 + hh
                                psq, psb = palloc(2)
                                for mb in range(2):
                                    S.op("pe", lambda e, mb=mb, psq=psq, h=h: e.matmul(
                                        psq[:, mb * P:(mb + 1) * P], lhsT=kM[:, h, mb * P:(mb + 1) * P], rhs=qM[:, h, qc], start=True, stop=True),
                                        reads=[b_kM, b_qM], writes=psb, sig=(mb == 1))
                                pi = ptstate["i"]; ptstate["i"] = (pi + 1) % NPT
                                S.op("act", lambda e, psq=psq, pi=pi: e.activation(
                                    out=pT[pi][:], in_=psq.rearrange("p (a n) -> p a n", n=P), func=AF.Exp, scale=128.0 ** -0.5),
                                    reads=psb, writes=[b_pT[pi]])
                                for mb in range(2):
                                    S.op("pe", lambda e, mb=mb, pi=pi, hh=hh, h=h, acc=acc: e.matmul(
                                        acc[:, hh * 136:hh * 136 + 130], lhsT=pT[pi][:, mb, :], rhs=vM[:, mb, h, 0:130], start=(mb == 0), stop=(mb == 1)),
                                        reads=[b_pT[pi], b_vM], writes=accb, sig=(mb == 1))
                        finish_branch(2, tl, accs, 128, 2, False)
                        stage(10)

                    stage(11)
                    for f in range(8):
                        wa, bwa = wload(wa3_d[:, f, :], 4608)
                        pps = []
                        for br in range(3):
                            pg, pgb = palloc(4)
                            for c in range(8):
                                S.op("pe", lambda e, c=c, br=br, pg=pg: e.matmul(
                                    pg, lhsT=wa[:, c * 384 + br * 128:c * 384 + (br + 1) * 128], rhs=hT[:, c, :], start=(c == 0), stop=(c == 7)),
                                    reads=[bwa] + b_hT, writes=pgb, sig=(c == 7))
                            S.op("act", lambda e, br=br, pg=pg: e.activation(out=gsb[br][:], in_=pg, func=AF.Sigmoid,
                                                                             bias=t_bgate[:, br * 8 + f:br * 8 + f + 1], scale=1.0),
                                 reads=pgb + [b_bgate], writes=[b_gsb[br]])
                            pp, ppb = palloc(4)
                            for c in range(4):
                                S.op("pe", lambda e, c=c, br=br, pp=pp: e.matmul(
                                    pp, lhsT=wa[:, 3072 + (br * 4 + c) * 128:3072 + (br * 4 + c + 1) * 128], rhs=oT[:, br * 4 + c, :],
                                    start=(c == 0), stop=(c == 3)),
                                    reads=[bwa] + b_oT[br], writes=ppb, sig=(c == 3))
                            S.op("dve", lambda e, br=br, pp=pp: e.tensor_tensor(out=tmg[br][:], in0=pp, in1=gsb[br][:], op=ALU.mult),
                                 reads=ppb + [b_gsb[br]], writes=[b_tmg[br]])
                        S.op("dve", lambda e: e.tensor_tensor(out=tmg[0][:], in0=tmg[0][:], in1=tmg[1][:], op=ALU.add),
                             reads=[b_tmg[0], b_tmg[1]], writes=[b_tmg[0]])
                        S.op("dve", lambda e, f=f: e.tensor_tensor(out=mT[:, f, :], in0=tmg[0][:], in1=tmg[2][:], op=ALU.add),
                             reads=[b_tmg[0], b_tmg[2]], writes=[b_mT[f]])
                    for tl in range(GT):
                        blk = g * GT + tl
                        r0 = tok0 + blk * P
                        xi = nxt("x1")
                        S.dma("sp", "dxr%d" % xi, lambda e, r0=r0, xi=xi: e.dma_start(out=x1t[xi][:], in_=x_d[r0:r0 + P, :]), writes=[b_x1t[xi]])
                        for hf in range(2):
                            py, pyb = palloc(4)
                            for f in range(8):
                                S.op("pe", lambda e, f=f, hf=hf, tl=tl, py=py: e.matmul(
                                    py, lhsT=mT[:, f, tl * P:(tl + 1) * P], rhs=t_wout[:, f, hf * 512:(hf + 1) * 512], start=(f == 0), stop=(f == 7)),
                                    reads=[b_mT[f], b_wout], writes=pyb, sig=(f == 7))
                            S.op("dve", lambda e, hf=hf, tl=tl, py=py, xi=xi: e.tensor_tensor(
                                out=x1t[xi][:, hf * 512:(hf + 1) * 512], in0=py, in1=x1t[xi][:, hf * 512:(hf + 1) * 512], op=ALU.add),
                                reads=pyb + [b_x1t[xi]], writes=[b_x1t[xi]])
                        S.dma("sp", "dst%d" % xi, lambda e, r0=r0, xi=xi: e.dma_start(out=out_d[r0:r0 + P, :], in_=x1t[xi][:]), reads=[b_x1t[xi]])
            stage(12)
            S.barrier()

          with contextlib.ExitStack() as sp_:
            wpq = sb(sp_, "wpq_s", [P, 8, 2048]); b_wpq = Buf("wpq")
            keysT = sb(sp_, "keys_s", [P, 16, P]); b_keys = Buf("keys")
            S.dma("sp", "dwpq", lambda e: e.dma_start(out=wpq[:], in_=wpq_d), writes=[b_wpq])
            S.dma("sp", "dwpq", lambda e: e.dma_start(out=keysT[:], in_=keys_d), writes=[b_keys])
            ring = [sb(sp_, "ring%d" % i, [P, D]) for i in range(RING)]; b_ring = [Buf("ring%d" % i) for i in range(RING)]
            rstate = {"i": 0}
            bc_reg = sp_.enter_context(nc.gpsimd.register("bcreg"))
            nc.gpsimd.reg_mov(bc_reg, nexp - 1)
            x1 = [sb(sp_, "px1_%d" % i, [P, D]) for i in range(2)]; b_x1 = [Buf("px1_0"), Buf("px1_1")]
            h2 = [sb(sp_, "ph2_%d" % i, [P, D]) for i in range(2)]; b_h2 = [Buf("ph2_0"), Buf("ph2_1")]
            acc = sb(sp_, "pacc", [P, D]); b_acc = Buf("pacc")
            g2bc = sb(sp_, "g2_s", [P, D]); b_g2 = Buf("g2")
            S.dma("sp", "dwpq", lambda e: e.dma_start(out=g2bc[:], in_=g2_d.partition_broadcast(P)), writes=[b_g2])
            junkd = sb(sp_, "junkD", [P, D]); b_junkd = Buf("junkD")
            ssp = sb(sp_, "ssP", [P, 1]); b_ssp = Buf("ssP")
            h2T = sb(sp_, "h2T", [P, 8, P]); b_h2T = Buf("h2T")
            qTs = sb(sp_, "qTs", [P, 16, P]); b_qTs = Buf("qTs")
            ssb = sb(sp_, "ssb", [P, 16, P]); b_ssb = Buf("ssb")
            rep = sb(sp_, "rep", [P, 256]); b_rep = Buf("rep")
            v16 = sb(sp_, "v16", [P, 16, 16]); b_v16 = Buf("v16")
            i16 = sb(sp_, "i16", [P, 16, 16], U32); b_i16 = Buf("i16")
            i16f = sb(sp_, "i16f", [P, 16, 16]); b_i16f = Buf("i16f")
            cand = sb(sp_, "cand", [P, 8, 16, 16]); b_cand = Buf("cand")
            svt = sb(sp_, "svt", [P, 8, 16]); b_sv = Buf("sv")
            sit = sb(sp_, "sit", [P, 8, 16], U32); b_si = Buf("si")
            ai_ = sb(sp_, "ai", [P, 8, 16], I32); b_ai = Buf("ai")
            af_ = sb(sp_, "af", [P, 8, 16]); b_af = Buf("af")
            bf_ = sb(sp_, "bff", [P, 8, 16]); b_bff = Buf("bff")
            mk = sb(sp_, "mk", [P, 8, 16, 16]); b_mk = Buf("mk")
            e1f = sb(sp_, "e1f", [P, 8, 16]); b_e1f = Buf("e1f")
            e2f = sb(sp_, "e2f", [P, 8, 16]); b_e2f = Buf("e2f")
            gsum = sb(sp_, "gsum", [P, 8]); b_gsum = Buf("gsum")
            idx = [sb(sp_, "idx%d" % i, [P, 128], I32) for i in range(2)]; b_idx = [Buf("idx0"), Buf("idx1")]
            gate = [sb(sp_, "gate%d" % i, [P, 8, 16]) for i in range(2)]; b_gate = [Buf("gate0"), Buf("gate1")]
            a_t = sb(sp_, "a_t", [P, 128]); b_a = Buf("a_t")
            act_t = sb(sp_, "act_t", [P, 128]); b_act = Buf("act_t")
            NT = NSEQ * NBLK if npeer is None else npeer

            def prep(t):
                pb_ = t % 2
                r0 = t * P
                S.dma("sp", "dpx%d" % pb_, lambda e: e.dma_start(out=x1[pb_][:], in_=out_d[r0:r0 + P, :]), writes=[b_x1[pb_]])
                rmsnorm(sp_, x1[pb_][:], b_x1[pb_], g2bc, b_g2, h2[pb_][:], b_h2[pb_], ssp, b_ssp)
                regs = transpose8(h2[pb_], b_h2[pb_])
                for hb, (pr, pbs) in enumerate(regs):
                    S.op("act", lambda e, hb=hb, pr=pr: e.copy(out=h2T[:, hb * 4:(hb + 1) * 4, :], in_=pr.rearrange("p (c n) -> p c n", n=P)),
                         reads=pbs, writes=[b_h2T])
                stage(21)
                for c4 in range(4):
                    pq, pqb = palloc(4)
                    for cc in range(4):
                        ch = c4 * 4 + cc
                        for c in range(8):
                            S.op("pe", lambda e, c=c, cc=cc, ch=ch, pq=pq: e.matmul(
                                pq[:, cc * P:(cc + 1) * P], lhsT=wpq[:, c, ch * P:(ch + 1) * P], rhs=h2T[:, c, :], start=(c == 0), stop=(c == 7)),
                                reads=[b_wpq, b_h2T], writes=pqb, sig=(c == 7 and cc == 3))
                    S.op("act", lambda e, c4=c4, pq=pq: e.copy(out=qTs[:, c4 * 4:(c4 + 1) * 4, :], in_=pq.rearrange("p (c n) -> p c n", n=P)),
                         reads=pqb, writes=[b_qTs])
                for c4 in range(4):
                    pq, pqb = palloc(4)
                    for cc in range(4):
                        ch = c4 * 4 + cc
                        S.op("pe", lambda e, cc=cc, ch=ch, pq=pq: e.matmul(
                            pq[:, cc * P:(cc + 1) * P], lhsT=qTs[:, ch, :], rhs=keysT[:, ch, :], start=True, stop=True),
                            reads=[b_qTs, b_keys], writes=pqb, sig=(cc == 3))
                    S.op("act", lambda e, c4=c4, pq=pq: e.copy(out=ssb[:, c4 * 4:(c4 + 1) * 4, :], in_=pq.rearrange("p (c n) -> p c n", n=P)),
                         reads=pqb, writes=[b_ssb])
                stage(22)
                for ch in range(16):
                    S.op("dve", lambda e, ch=ch: e.max(out=v16[:, ch, 0:8], in_=ssb[:, ch, :]), reads=[b_ssb], writes=[b_v16])
                    S.op("dve", lambda e, ch=ch: e.match_replace(out=rep[:, 0:128], in_to_replace=v16[:, ch, 0:8], in_values=ssb[:, ch, :], imm_value=-1e30),
                         reads=[b_ssb, b_v16], writes=[b_rep])
                    S.op("dve", lambda e, ch=ch: e.max(out=v16[:, ch, 8:16], in_=rep[:, 0:128]), reads=[b_rep], writes=[b_v16])
                    S.op("dve", lambda e, ch=ch: e.max_index(out=i16[:, ch, 0:8], in_max=v16[:, ch, 0:8], in_values=ssb[:, ch, :]),
                         reads=[b_ssb, b_v16], writes=[b_i16])
                    S.op("dve", lambda e, ch=ch: e.max_index(out=i16[:, ch, 8:16], in_max=v16[:, ch, 8:16], in_values=ssb[:, ch, :]),
                         reads=[b_ssb, b_v16], writes=[b_i16])
                v16v = v16[:].rearrange("p (h two) k -> p h two k", two=2)
                S.op("dve", lambda e: e.tensor_tensor(out=cand[:], in0=v16v[:, :, 0, :].unsqueeze(3).to_broadcast([P, 8, 16, 16]),
                                                      in1=v16v[:, :, 1, :].unsqueeze(2).to_broadcast([P, 8, 16, 16]), op=ALU.add),
                     reads=[b_v16], writes=[b_cand])
                for h in range(8):
                    ch_ = cand[:, h, :, :].rearrange("p a b -> p (a b)")
                    S.op("dve", lambda e, h=h, ch_=ch_: e.max(out=svt[:, h, 0:8], in_=ch_), reads=[b_cand], writes=[b_sv])
                    S.op("dve", lambda e, h=h, ch_=ch_: e.match_replace(out=rep[:], in_to_replace=svt[:, h, 0:8], in_values=ch_, imm_value=-1e30),
                         reads=[b_cand, b_sv], writes=[b_rep])
                    S.op("dve", lambda e, h=h: e.max(out=svt[:, h, 8:16], in_=rep[:]), reads=[b_rep], writes=[b_sv])
                    S.op("dve", lambda e, h=h, ch_=ch_: e.max_index(out=sit[:, h, 0:8], in_max=svt[:, h, 0:8], in_values=ch_),
                         reads=[b_cand, b_sv], writes=[b_si])
                    S.op("dve", lambda e, h=h, ch_=ch_: e.max_index(out=sit[:, h, 8:16], in_max=svt[:, h, 8:16], in_values=ch_),
                         reads=[b_cand, b_sv], writes=[b_si])
                stage(23)
                S.op("dve", lambda e: e.tensor_copy(out=i16f[:], in_=i16[:]), reads=[b_i16], writes=[b_i16f])
                S.op("dve", lambda e: e.tensor_single_scalar(out=ai_[:], in_=sit[:].bitcast(I32), scalar=4, op=ALU.logical_shift_right),
                     reads=[b_si], writes=[b_ai])
                S.op("dve", lambda e: e.tensor_copy(out=af_[:], in_=ai_[:]), reads=[b_ai], writes=[b_af])
                S.op("dve", lambda e: e.tensor_single_scalar(out=ai_[:], in_=sit[:].bitcast(I32), scalar=15, op=ALU.bitwise_and),
                     reads=[b_si, b_af], writes=[b_ai])
                S.op("dve", lambda e: e.tensor_copy(out=bf_[:], in_=ai_[:]), reads=[b_ai], writes=[b_bff])
                i16v = i16f[:].rearrange("p (h two) k -> p h two k", two=2)
                iob = iota16[:].unsqueeze(1).unsqueeze(1).to_broadcast([P, 8, 16, 16])
                for which, (src, bsrc, dst, bdst) in enumerate(((af_, b_af, e1f, b_e1f), (bf_, b_bff, e2f, b_e2f))):
                    S.op("dve", lambda e, src=src: e.tensor_tensor(out=mk[:], in0=src[:].unsqueeze(3).to_broadcast([P, 8, 16, 16]), in1=iob, op=ALU.is_equal),
                         reads=[bsrc, b_iota], writes=[b_mk])
                    S.op("dve", lambda e, which=which: e.tensor_tensor(
                        out=mk[:], in0=mk[:], in1=i16v[:, :, which, :].unsqueeze(2).to_broadcast([P, 8, 16, 16]), op=ALU.mult),
                        reads=[b_mk, b_i16f], writes=[b_mk])
                    S.op("dve", lambda e, dst=dst: e.tensor_reduce(out=dst[:], in_=mk[:], axis=AX.X, op=ALU.add), reads=[b_mk], writes=[bdst])
                S.op("dve", lambda e: e.scalar_tensor_tensor(out=e1f[:], in0=e1f[:], scalar=128.0, in1=e2f[:], op0=ALU.mult, op1=ALU.add),
                     reads=[b_e1f, b_e2f], writes=[b_e1f])
                S.op("dve", lambda e: e.tensor_copy(out=idx[pb_][:], in_=e1f[:].rearrange("p h k -> p (h k)")), reads=[b_e1f], writes=[b_idx[pb_]])
                stage(24)
                S.op("dve", lambda e: e.tensor_tensor(out=gate[pb_][:], in0=svt[:], in1=svt[:, :, 0:1].to_broadcast([P, 8, 16]), op=ALU.subtract),
                     reads=[b_sv], writes=[b_gate[pb_]])
                S.op("act", lambda e: e.activation(out=gate[pb_][:], in_=gate[pb_][:], func=AF.Exp), reads=[b_gate[pb_]], writes=[b_gate[pb_]])
                S.op("dve", lambda e: e.tensor_reduce(out=gsum[:], in_=gate[pb_][:], axis=AX.X, op=ALU.add), reads=[b_gate[pb_]], writes=[b_gsum])
                S.op("dve", lambda e: e.reciprocal(out=gsum[:], in_=gsum[:]), reads=[b_gsum], writes=[b_gsum])
                S.op("dve", lambda e: e.tensor_tensor(out=gate[pb_][:], in0=gate[pb_][:], in1=gsum[:].unsqueeze(2).to_broadcast([P, 8, 16]), op=ALU.mult),
                     reads=[b_gate[pb_], b_gsum], writes=[b_gate[pb_]])

            def gather(tab, col_ap, bidx):
                r = rstate["i"]; rstate["i"] = (r + 1) % RING
                S.dma("pool", "dg%d" % r, lambda e: e.indirect_dma_start(
                    out=ring[r][:], out_offset=None, in_=tab, in_offset=bass.IndirectOffsetOnAxis(ap=col_ap, axis=0),
                    bounds_check=bc_reg, oob_is_err=False),
                    reads=[bidx], writes=[b_ring[r]])
                return r

            def consume(t):
                pb_ = t % 2
                r0 = t * P
                stage(25)
                S.op("dve", lambda e: e.memset(a_t[:], 0.0), writes=[b_a])
                for s in range(128):
                    r = gather(u_d, idx[pb_][:, s:s + 1], b_idx[pb_])
                    S.op("dve", lambda e, s=s, r=r: e.scalar_tensor_tensor(
                        out=junkd[:], in0=ring[r][:], scalar=1.0, in1=h2[pb_][:], op0=ALU.mult, op1=ALU.mult, accum_out=a_t[:, s:s + 1]),
                        reads=[b_ring[r], b_h2[pb_]], writes=[b_junkd, b_a])
                stage(26)
                S.op("act", lambda e: e.activation(out=act_t[:], in_=a_t[:], func=AF.Gelu), reads=[b_a], writes=[b_act])
                S.op("dve", lambda e: e.tensor_tensor(out=act_t[:], in0=act_t[:], in1=gate[pb_][:].rearrange("p h k -> p (h k)"), op=ALU.mult),
                     reads=[b_act, b_gate[pb_]], writes=[b_act])
                for s in range(128):
                    r = gather(v_d, idx[pb_][:, s:s + 1], b_idx[pb_])
                    if s == 0:
                        S.op("dve", lambda e, r=r: e.tensor_scalar(out=acc[:], in0=ring[r][:], scalar1=act_t[:, 0:1], scalar2=None, op0=ALU.mult),
                             reads=[b_ring[r], b_act], writes=[b_acc])
                    else:
                        S.op("dve", lambda e, s=s, r=r: e.scalar_tensor_tensor(
                            out=acc[:], in0=ring[r][:], scalar=act_t[:, s:s + 1], in1=acc[:], op0=ALU.mult, op1=ALU.add),
                            reads=[b_ring[r], b_act, b_acc], writes=[b_acc])
                S.op("dve", lambda e: e.tensor_tensor(out=acc[:], in0=acc[:], in1=x1[pb_][:], op=ALU.add), reads=[b_acc, b_x1[pb_]], writes=[b_acc])
                S.dma("sp", "dpo", lambda e: e.dma_start(out=out_d[r0:r0 + P, :], in_=acc[:]), reads=[b_acc])

            if NT > 0:
                prep(0)
            for t in range(NT):
                if t + 1 < NT:
                    prep(t + 1)
                consume(t)
        except _StopBuild:
            pass
        S.finish()
    return nc


def _kmajor(w):
    K, N = w.shape
    return np.ascontiguousarray(w.reshape(K // P, P, N).transpose(1, 0, 2))


def _host_layout(inp):
    f32 = np.float32
    w_in = inp["w_in"][0]
    o = 0
    parts = []
    for wdt in (512, 512, 512, 8, 512, 128, 128, 512, 3072):
        parts.append(w_in[:, o:o + wdt]); o += wdt
    fq, fk, fv, ffw, sqw, skw, svw, mqw, glw = parts
    sq_perm = np.concatenate([np.concatenate([sqw[:, c * 64:(c + 1) * 64], sqw[:, (c + 4) * 64:(c + 5) * 64]], axis=1) for c in range(4)], axis=1)
    slot4 = np.zeros((D, 512), f32)
    slot4[:, 0:128] = skw
    slot4[:, 128:256] = svw
    wa1 = np.stack([_kmajor(m) for m in (fq, fk, sq_perm, mqw, slot4, fv)], axis=1)
    wmkv = inp["w_mem_kv"][0]
    wmkv_t = np.stack([_kmajor(wmkv[:, 0:512]), _kmajor(wmkv[:, 512:1024])], axis=1)
    wff = _kmajor(ffw)
    glk = _kmajor(glw)
    wo = np.stack([_kmajor(inp[k][0]) for k in ("w_fox_o", "w_swa_o", "w_mem_o")], axis=1)
    wa3 = np.zeros((P, 8, 4608), f32)
    for f in range(8):
        gpart = np.stack([glk[:, :, br * 1024 + f * 128: br * 1024 + (f + 1) * 128] for br in range(3)], axis=2)
        wa3[:, f, 0:3072] = gpart.reshape(P, 3072)
        wa3[:, f, 3072:] = wo[:, :, :, f * 128:(f + 1) * 128].reshape(P, 12 * 128)
    wout = _kmajor(inp["w_out"][0])
    wpq = _kmajor(inp["w_peer_q"][0])
    k1 = inp["peer_keys1"][0]
    k2 = inp["peer_keys2"][0]
    keysT = np.zeros((P, 16, P), f32)
    for h in range(8):
        keysT[:, 2 * h, :] = k1[h].T
        keysT[:, 2 * h + 1, :] = k2[h].T
    gcols = np.zeros((P, 18), f32)
    two = lambda g: np.concatenate([g, g])
    for c in range(4):
        gcols[:, c] = two(inp["fox_q_g"][0])
        gcols[:, 4 + c] = two(inp["fox_k_g"][0])
        gcols[:, 8 + c] = two(inp["swa_q_g"][0])
        gcols[:, 12 + c] = inp["mem_q_g"][0]
    gcols[:, 16] = two(inp["swa_k_g"][0])
    gcols[:, 17] = inp["mem_k_g"][0]
    bgate = np.ascontiguousarray(inp["b_gate"][0].reshape(24, P).T)
    kk = np.arange(P)[:, None]
    qq = np.arange(P)[None, :]
    blk = (kk // 64 == qq // 64)
    slopes = 2.0 ** (-(np.arange(8, dtype=np.float64) + 1.0))
    eb = np.zeros((P, 8, 2, P), f32)
    for h in range(8):
        eb[:, h, 0, :] = np.where(kk > qq, np.exp(-slopes[h] * (qq + P - kk)), 0.0)
        eb[:, h, 1, :] = np.where(kk <= qq, np.exp(-slopes[h] * (qq - kk)), 0.0)
    bf16 = ml_dtypes.bfloat16
    shared = dict(
        wa1=wa1, wmkv=wmkv_t, wff=wff, wa3=wa3, wout=wout, wpq=wpq, keysT=keysT,
        peer_u=np.ascontiguousarray(inp["peer_u"][0]), peer_v=np.ascontiguousarray(inp["peer_v"][0]),
        g1=inp["norm1_g"].reshape(1, D), g2=inp["norm2_g"].reshape(1, D), gm=inp["mem_norm_g"].reshape(1, D),
        bforget=inp["b_forget"].reshape(1, 8), sinks=inp["swa_sinks"].reshape(1, 8),
        gcols=gcols, bgate=bgate,
        identf=np.eye(P, dtype=f32), identb=np.eye(P, dtype=f32).astype(bf16),
        onesblk=blk.astype(f32).astype(bf16), onesfull=np.ones((P, P), f32).astype(bf16),
        tri=(kk <= qq).astype(f32), onesf=np.ones((P, P), f32),
        cmask=(kk <= qq).astype(f32).astype(bf16), eb=eb,
        iota16=np.broadcast_to(np.arange(16, dtype=f32), (P, 16)).copy(),
    )
    return {k: np.ascontiguousarray(v) for k, v in shared.items()}


def kernel(**inputs):
    inp = {k: np.asarray(v) for k, v in inputs.items()}
    shared = _host_layout(inp)
    x = np.ascontiguousarray(inp["x"], dtype=np.float32)
    mem = np.ascontiguousarray(inp["mem"], dtype=np.float32)
    in_maps = []
    for c in range(NCORES):
        m = dict(shared)
        m["x"] = x[c * NSEQ:(c + 1) * NSEQ].reshape(NSEQ * SEQ, D)
        m["mem"] = mem[c * NSEQ:(c + 1) * NSEQ].reshape(NSEQ * NMEM, D)
        in_maps.append(m)
    nc = build_program()
    res = run_bass_kernel_spmd(nc, in_maps, core_ids=list(range(NCORES)))
    outs = [np.asarray(r["out"]).reshape(NSEQ, SEQ, D) for r in res.results]
    return np.concatenate(outs, axis=0).astype(np.float32)
```

```python
import contextlib
import numpy as np
import ml_dtypes
import concourse.bass as bass
import concourse.mybir as mybir
from concourse.bass_utils import run_bass_kernel_spmd

F32 = mybir.dt.float32
BF16 = mybir.dt.bfloat16
I32 = mybir.dt.int32
U32 = mybir.dt.uint32
ALU = mybir.AluOpType
AF = mybir.ActivationFunctionType
AX = mybir.AxisListType

P = 128
D = 1024
NCORES = 8
SEQ = 2048
NSEQ = 2
NBLK = SEQ // P
NGRP = 4
GT = 4
GN = GT * P
NMEM = 256
EPS = 1e-6
NEXP = 16384
RING = 8


class Buf:
    __slots__ = ("name", "w", "r", "excl")

    def __init__(self, name, excl=False):
        self.name = name
        self.w = None
        self.r = {}
        self.excl = excl


class Sched:
    ENGS = ("pe", "act", "dve", "pool", "sp")

    def __init__(self, nc, stack):
        self.nc = nc
        self.stack = stack
        self.eng = {"pe": nc.tensor, "act": nc.scalar, "dve": nc.vector,
                    "pool": nc.gpsimd, "sp": nc.sync}
        self.sem = {}
        self.cnt = {}
        self.waited = {e: {} for e in self.ENGS}
        self.owner = {}
        self.stopped = [False]
        for e in self.ENGS:
            self.sem[e] = stack.enter_context(nc.semaphore("s_" + e))
            self.cnt[e] = 0

    def _wait(self, e, key, val):
        if key in self.ENGS:
            if key == e:
                if e in ("pe", "sp"):
                    return
                if val < self.cnt[e] - 1:
                    return
        else:
            val = self.cnt[key]
        if self.waited[e].get(key, 0) >= val:
            return
        self.waited[e][key] = val
        self.eng[e].wait_ge(self.sem[key], val)

    def _deps(self, e, reads, writes):
        deps = {}
        for b in reads:
            if b.w is not None:
                k, v = b.w
                if deps.get(k, 0) < v:
                    deps[k] = v
        for b in writes:
            if b.w is not None:
                k, v = b.w
                if deps.get(k, 0) < v:
                    deps[k] = v
            for k, v in b.r.items():
                if deps.get(k, 0) < v:
                    deps[k] = v
        for k, v in deps.items():
            self._wait(e, k, v)

    def _mark(self, ticket, reads, writes):
        k, v = ticket
        for b in reads:
            if b.r.get(k, 0) < v:
                b.r[k] = v
        for b in writes:
            b.w = ticket
            b.r = {}

    def op(self, e, fn, reads=(), writes=(), sig=True):
        if self.stopped[0]:
            return None
        if any(b.excl for b in reads):
            writes = list(writes) + [b for b in reads if b.excl]
            reads = [b for b in reads if not b.excl]
        self._deps(e, reads, writes)
        inst = fn(self.eng[e])
        if sig:
            inst.then_inc(self.sem[e], 1)
            self.cnt[e] += 1
            ticket = (e, self.cnt[e])
        else:
            ticket = (e, self.cnt[e] + 1)
        self._mark(ticket, reads, writes)
        return inst

    def dma(self, q, semkey, fn, reads=(), writes=()):
        if self.stopped[0]:
            return None
        if semkey not in self.sem:
            self.sem[semkey] = self.stack.enter_context(self.nc.semaphore("s_" + semkey))
            self.cnt[semkey] = 0
            self.owner[semkey] = q
        assert self.owner[semkey] == q
        self._deps(q, reads, writes)
        inst = fn(self.eng[q])
        inst.then_inc(self.sem[semkey], 16)
        self.cnt[semkey] += 16
        self._mark((semkey, self.cnt[semkey]), reads, writes)
        return inst

    def barrier(self):
        if self.stopped[0]:
            return
        for e in self.ENGS:
            for k in list(self.sem):
                if k != e and self.cnt[k] > 0:
                    v = self.cnt[k]
                    if self.waited[e].get(k, 0) < v:
                        self.waited[e][k] = v
                        self.eng[e].wait_ge(self.sem[k], v)

    def finish(self):
        for k in list(self.sem):
            if k != "sp" and self.cnt[k] > 0:
                v = self.cnt[k]
                if self.waited["sp"].get(k, 0) < v:
                    self.waited["sp"][k] = v
                    self.eng["sp"].wait_ge(self.sem[k], v)


class _StopBuild(Exception):
    pass


def build_program(nseq=NSEQ, ngrp=NGRP, npeer=None, phase_a=True, stop_at=None, nexp=NEXP):
    _stopped = [False]
    def stage(n):
        if stop_at is not None and n >= stop_at:
            _stopped[0] = True

    nc = bass.Bass("TRN2", target_bir_lowering=False)

    def din(name, shape, dt=F32):
        return nc.dram_tensor(name, list(shape), dt, kind="ExternalInput").ap()

    x_d = din("x", [NSEQ * SEQ, D])
    mem_d = din("mem", [NSEQ * NMEM, D])
    out_d = nc.dram_tensor("out", [NSEQ * SEQ, D], F32, kind="ExternalOutput").ap()
    wa1_d = din("wa1", [P, 6, 8, 512])
    wmkv_d = din("wmkv", [P, 2, 8, 512])
    wff_d = din("wff", [P, 8, 8])
    wa3_d = din("wa3", [P, 8, 4608])
    wout_d = din("wout", [P, 8, D])
    wpq_d = din("wpq", [P, 8, 2048])
    keys_d = din("keysT", [P, 16, P])
    u_d = din("peer_u", [nexp, D])
    v_d = din("peer_v", [nexp, D])
    g1_d = din("g1", [1, D])
    g2_d = din("g2", [1, D])
    gm_d = din("gm", [1, D])
    bf_d = din("bforget", [1, 8])
    sink_d = din("sinks", [1, 8])
    gcols_d = din("gcols", [P, 18])
    bgate_d = din("bgate", [P, 24])
    identf_d = din("identf", [P, P])
    identb_d = din("identb", [P, P], BF16)
    onesblk_d = din("onesblk", [P, P], BF16)
    onesfull_d = din("onesfull", [P, P], BF16)
    tri_d = din("tri", [P, P])
    onesf_d = din("onesf", [P, P])
    cmask_d = din("cmask", [P, P], BF16)
    eb_d = din("eb", [P, 8, 2, P])
    iota_d = din("iota16", [P, 16])

    with contextlib.ExitStack() as st:
        S = Sched(nc, st)
        S.stopped = _stopped

        def sb(stack, name, shape, dt=F32):
            return stack.enter_context(nc.sbuf_tensor(name, list(shape), dt))

        banks = [st.enter_context(nc.psum_tensor("pb%d" % i, [P, 512], F32)) for i in range(8)]
        bbufs = [Buf("bank%d" % i, excl=True) for i in range(8)]
        pstate = {"L": 0, "S": 0}

        def palloc(n):
            if n == 4:
                b = pstate["L"]
                pstate["L"] = (b + 1) % 4
            else:
                b = 4 + pstate["S"]
                pstate["S"] = (pstate["S"] + 1) % 4
            return banks[b][:, 0:n * 128], [bbufs[b]]

        def psb_full(bl):
            return banks[bbufs.index(bl[0])][:, 0:512]

        def cload(name, src, shape, dt=F32, bc=False):
            t = sb(st, name, shape, dt)
            b = Buf(name)
            S.dma("sp", "dconst", lambda e: e.dma_start(out=t[:], in_=(src.partition_broadcast(P) if bc else src)), writes=[b])
            return t, b

        identf, b_identf = cload("identf_s", identf_d, [P, P])
        identb, b_identb = cload("identb_s", identb_d, [P, P], BF16)
        g1bc, b_g1 = cload("g1_s", g1_d, [P, D], bc=True)
        iota16, b_iota = cload("iota_s", iota_d, [P, 16])
        CB = [b_identf, b_identb, b_g1, b_iota]

        def rmsnorm(eng_stack, xt, bx, gbc, bg, ht, bh, ss, bss):
            S.op("dve", lambda e: e.memset(ss[:], 0.0), writes=[bss])
            S.op("act", lambda e: e.activation(out=ht, in_=xt, func=AF.Square, accum_out=ss[:]),
                 reads=[bx], writes=[bh, bss])
            S.op("act", lambda e: e.activation(out=ss[:], in_=ss[:], func=AF.Sqrt, bias=EPS, scale=1.0 / D),
                 reads=[bss], writes=[bss])
            S.op("dve", lambda e: e.reciprocal(out=ss[:], in_=ss[:]), reads=[bss], writes=[bss])
            S.op("dve", lambda e: e.scalar_tensor_tensor(out=ht, in0=xt, scalar=ss[:, 0:1], in1=gbc[:],
                                                          op0=ALU.mult, op1=ALU.mult),
                 reads=[bx, bss, bg], writes=[bh])

        def transpose8(ht, bh):
            regs = []
            for hb in range(2):
                pr, pbs = palloc(4)
                for c4 in range(4):
                    c = hb * 4 + c4
                    S.op("pe", lambda e, c=c, c4=c4, pr=pr: e.transpose(out=pr[:, c4 * 128:(c4 + 1) * 128],
                                                                      in_=ht[:, c * 128:(c + 1) * 128], identity=identf[:]),
                         reads=[bh, b_identf], writes=pbs, sig=(c4 == 3))
                regs.append((pr, pbs))
            return regs

        try:
          with contextlib.ExitStack() as sa:
            onesblk, b_onesblk = None, None
            t_onesblk = sb(sa, "onesblk_s", [P, P], BF16); b_onesblk = Buf("onesblk")
            t_onesfull = sb(sa, "onesfull_s", [P, P], BF16); b_onesfull = Buf("onesfull")
            t_tri = sb(sa, "tri_s", [P, P]); b_tri = Buf("tri")
            t_onesf = sb(sa, "onesf_s", [P, P]); b_onesf = Buf("onesf")
            t_cmask = sb(sa, "cmask_s", [P, P], BF16); b_cmask = Buf("cmask")
            t_eb = sb(sa, "eb_s", [P, 8, 2, P]); b_eb = Buf("eb")
            t_gm = sb(sa, "gm_s", [P, D]); b_gm = Buf("gm")
            t_bf = sb(sa, "bf_s", [P, 8]); b_bf = Buf("bf")
            t_sink = sb(sa, "sink_s", [P, 8]); b_sink = Buf("sink")
            t_gcols = sb(sa, "gcols_s", [P, 18]); b_gcols = Buf("gcols")
            t_bgate = sb(sa, "bgate_s", [P, 24]); b_bgate = Buf("bgate")
            t_wff = sb(sa, "wff_s", [P, 8, 8]); b_wff = Buf("wff")
            t_wout = sb(sa, "wout_s", [P, 8, D], BF16); b_wout = Buf("wout")
            for t, b, src, bc in ((t_onesblk, b_onesblk, onesblk_d, False), (t_onesfull, b_onesfull, onesfull_d, False),
                                  (t_tri, b_tri, tri_d, False), (t_onesf, b_onesf, onesf_d, False),
                                  (t_cmask, b_cmask, cmask_d, False), (t_eb, b_eb, eb_d, False),
                                  (t_gm, b_gm, gm_d, True), (t_bf, b_bf, bf_d, True), (t_sink, b_sink, sink_d, True),
                                  (t_gcols, b_gcols, gcols_d, False), (t_bgate, b_bgate, bgate_d, False),
                                  (t_wff, b_wff, wff_d, False)):
                S.dma("sp", "dconst", lambda e, t=t, src=src, bc=bc: e.dma_start(
                    out=t[:], in_=(src.partition_broadcast(P) if bc else src)), writes=[b])
            S.dma("pool", "dwout", lambda e: e.dma_start(out=t_wout[:], in_=wout_d), writes=[b_wout])
            S.op("act", lambda e: e.activation(out=t_sink[:], in_=t_sink[:], func=AF.Exp), reads=[b_sink], writes=[b_sink])

            stage(1)
            NW = 3
            wring = [sb(sa, "wring%d" % i, [P, 4608], BF16) for i in range(NW)]
            b_wring = [Buf("wring%d" % i) for i in range(NW)]
            wstate = {"i": 0}

            def wload(src_ap, ncols):
                i = wstate["i"]
                wstate["i"] = (i + 1) % NW
                ov = wring[i][:, 0:ncols]
                if len(src_ap.shape) == 3:
                    ov = ov.rearrange("p (c n) -> p c n", n=src_ap.shape[2])
                S.dma("pool", "dw%d" % i, lambda e: e.dma_start(out=ov, in_=src_ap), writes=[b_wring[i]])
                return wring[i], b_wring[i]

            kF = sb(sa, "kF", [P, 4, SEQ], BF16); b_kF = [Buf("kF%d" % i) for i in range(NGRP)]
            vF = sb(sa, "vF", [P, NBLK, 8, 72], BF16); b_vF = [Buf("vF%d" % i) for i in range(NBLK)]
            kS = sb(sa, "kS", [P, SEQ], BF16); b_kS = [Buf("kS%d" % i) for i in range(NGRP)]
            vS = sb(sa, "vS", [P, NBLK, 2, 72], BF16); b_vS = [Buf("vS%d" % i) for i in range(NBLK)]
            kM = sb(sa, "kM", [P, 4, NMEM], BF16); b_kM = Buf("kM")
            vM = sb(sa, "vM", [P, 2, 4, 136], BF16); b_vM = Buf("vM")
            S.op("dve", lambda e: e.memset(vF[:], 1.0), writes=b_vF)
            S.op("dve", lambda e: e.memset(vS[:], 1.0), writes=b_vS)
            S.op("dve", lambda e: e.memset(vM[:], 1.0), writes=[b_vM])
            Cc = sb(sa, "Cc", [P, 8, NBLK]); b_Cc = Buf("Cc")
            Off = sb(sa, "Off", [P, 8, NBLK]); b_Off = Buf("Off")
            carry = [sb(sa, "carry%d" % i, [P, 8]) for i in range(2)]
            b_carry = [Buf("carry0"), Buf("carry1")]
            xg = sb(sa, "xg", [P, 2, D]); b_xg = [Buf("xg%d" % i) for i in range(2)]
            hT = sb(sa, "hT", [P, 8, GN], BF16); b_hT = [Buf("hT%d" % i) for i in range(GT)]
            hT32 = sb(sa, "hT32", [P, 8, P]); b_hT32 = Buf("hT32")
            h_t = sb(sa, "h_t", [P, D]); b_h = Buf("h")
            ss = sb(sa, "ssA", [P, 1]); b_ss = Buf("ssA")
            qF = sb(sa, "qF", [P, 4, GN], BF16); b_qF = Buf("qF")
            qS = sb(sa, "qS", [P, 4, GN], BF16); b_qS = Buf("qS")
            qM = sb(sa, "qM", [P, 4, GN], BF16); b_qM = Buf("qM")
            ffs = sb(sa, "ffs", [P, GT, 8]); b_ffs = Buf("ffs")
            spl = sb(sa, "spl", [P, GT, 8]); b_spl = Buf("spl")
            sq = [sb(sa, "sq%d" % i, [P, GN], BF16) for i in range(2)]; b_sq = [Buf("sq0"), Buf("sq1")]
            rs = [sb(sa, "rs%d" % i, [P, GN]) for i in range(2)]; b_rs = [Buf("rs0"), Buf("rs1")]
            NPT = 4
            pT = [sb(sa, "pT%d" % i, [P, 4, P], BF16) for i in range(NPT)]; b_pT = [Buf("pT%d" % i) for i in range(NPT)]
            ptstate = {"i": 0}
            esw = [sb(sa, "esw%d" % i, [P, 4, P]) for i in range(2)]; b_esw = [Buf("esw0"), Buf("esw1")]
            biasF = [sb(sa, "biasF%d" % i, [P, NBLK]) for i in range(2)]; b_biasF = [Buf("biasF0"), Buf("biasF1")]
            rec = [sb(sa, "rec%d" % i, [P, 4]) for i in range(2)]; b_rec = [Buf("rec0"), Buf("rec1")]
            otok = [sb(sa, "otok%d" % i, [P, 512], BF16) for i in range(2)]; b_otok = [Buf("otok0"), Buf("otok1")]
            oT = sb(sa, "oT", [P, 12, GN], BF16); b_oT = [[Buf("oT%d_%d" % (br, i)) for i in range(GT)] for br in range(3)]
            gsb = [sb(sa, "gsb%d" % i, [P, GN], BF16) for i in range(3)]; b_gsb = [Buf("gsb%d" % i) for i in range(3)]
            tmg = [sb(sa, "tmg%d" % i, [P, GN]) for i in range(3)]; b_tmg = [Buf("tmg%d" % i) for i in range(3)]
            mT = sb(sa, "mT", [P, 8, GN], BF16); b_mT = [Buf("mT%d" % i) for i in range(8)]
            x1t = [sb(sa, "x1t%d" % i, [P, D]) for i in range(2)]; b_x1t = [Buf("x1t0"), Buf("x1t1")]
            cnts = {"sq": 0, "esw": 0, "bias": 0, "rec": 0, "otok": 0, "x1": 0}

            def nxt(k, n=2):
                v = cnts[k]
                cnts[k] = (v + 1) % n
                return v

            def proj_fm(wsl, bw, s4, rhs, brhs, N, dest, bdest, ones_t, b_ones, hd, gcol):
                pz, pzb = palloc(4)
                for c in range(8):
                    S.op("pe", lambda e, c=c: e.matmul(pz[:, 0:N], lhsT=wsl[:, c * 512 + s4 * 128: c * 512 + (s4 + 1) * 128],
                                                        rhs=rhs[:, c, :], start=(c == 0), stop=(c == 7)),
                         reads=[bw] + brhs, writes=pzb, sig=(c == 7))
                k = nxt("sq")
                S.op("act", lambda e: e.activation(out=sq[k][:, 0:N], in_=pz[:, 0:N], func=AF.Square), reads=pzb, writes=[b_sq[k]])
                pss, pssb = palloc(4)
                S.op("pe", lambda e: e.matmul(pss[:, 0:N], lhsT=ones_t[:], rhs=sq[k][:, 0:N], start=True, stop=True),
                     reads=[b_sq[k], b_ones], writes=pssb)
                S.op("act", lambda e: e.activation(out=rs[k][:, 0:N], in_=pss[:, 0:N], func=AF.Sqrt, bias=EPS, scale=1.0 / hd),
                     reads=pssb, writes=[b_rs[k]])
                S.op("dve", lambda e: e.reciprocal(out=rs[k][:, 0:N], in_=rs[k][:, 0:N]), reads=[b_rs[k]], writes=[b_rs[k]])
                S.op("dve", lambda e: e.scalar_tensor_tensor(out=dest, in0=pz[:, 0:N], scalar=gcol, in1=rs[k][:, 0:N],
                                                              op0=ALU.mult, op1=ALU.mult),
                     reads=pzb + [b_rs[k], b_gcols], writes=bdest)

            for sq_i in range(nseq if phase_a else 0):
                tok0 = sq_i * SEQ
                mnT = hT
                for mb in range(2):
                    r0 = sq_i * NMEM + mb * P
                    S.dma("sp", "dx0", lambda e, r0=r0: e.dma_start(out=xg[:, 0, :], in_=mem_d[r0:r0 + P, :]), writes=[b_xg[0]])
                    rmsnorm(sa, xg[:, 0, :], b_xg[0], t_gm, b_gm, h_t[:], b_h, ss, b_ss)
                    regs = transpose8(h_t, b_h)
                    for hb, (pr, pbs) in enumerate(regs):
                        S.op("act", lambda e, hb=hb, pr=pr, mb=mb: e.copy(
                            out=mnT[:, hb * 4:(hb + 1) * 4, mb * P:(mb + 1) * P],
                            in_=pr.rearrange("p (c n) -> p c n", n=P)), reads=pbs, writes=[b_hT[mb]])
                stage(2)
                wk, bwk = wload(wmkv_d[:, 0, :, :], 4096)
                stage(2.3)
                for hd_i in range(4):
                    proj_fm(wk, bwk, hd_i, mnT[:, :, 0:NMEM], [b_hT[0], b_hT[1]], NMEM, kM[:, hd_i, :], [b_kM],
                            t_onesfull, b_onesfull, 128.0, t_gcols[:, 17:18])
                stage(2.6)
                wv, bwv = wload(wmkv_d[:, 1, :, :], 4096)
                for mb in range(2):
                    pv, pvb = palloc(4)
                    for c in range(8):
                        S.op("pe", lambda e, c=c, mb=mb: e.matmul(pv, lhsT=mnT[:, c, mb * P:(mb + 1) * P], rhs=wv[:, c * 512:(c + 1) * 512],
                                                                 start=(c == 0), stop=(c == 7)),
                             reads=[bwv, b_hT[mb]], writes=pvb, sig=(c == 7))
                    S.op("act", lambda e, mb=mb, pv=pv: e.copy(out=vM[:, mb, :, 0:128], in_=pv.rearrange("p (h d) -> p h d", d=128)),
                         reads=pvb, writes=[b_vM])
                stage(3)
                S.op("dve", lambda e: e.memset(carry[0][:], 0.0), writes=[b_carry[0]])
                cstate = 0

                for g in range(ngrp):
                    for tl in range(GT):
                        blk = g * GT + tl
                        r0 = tok0 + blk * P
                        xb_ = tl % 2
                        S.dma("sp", "dx%d" % xb_, lambda e, r0=r0, xb_=xb_: e.dma_start(out=xg[:, xb_, :], in_=x_d[r0:r0 + P, :]), writes=[b_xg[xb_]])
                        rmsnorm(sa, xg[:, xb_, :], b_xg[xb_], g1bc, b_g1, h_t[:], b_h, ss, b_ss)
                        regs = transpose8(h_t, b_h)
                        for hb, (pr, pbs) in enumerate(regs):
                            S.op("act", lambda e, hb=hb, pr=pr, tl=tl: e.copy(
                                out=hT[:, hb * 4:(hb + 1) * 4, tl * P:(tl + 1) * P],
                                in_=pr.rearrange("p (c n) -> p c n", n=P)), reads=pbs, writes=[b_hT[tl]])
                            S.op("act", lambda e, hb=hb, pr=pr: e.copy(
                                out=hT32[:, hb * 4:(hb + 1) * 4, :], in_=pr.rearrange("p (c n) -> p c n", n=P)),
                                reads=pbs, writes=[b_hT32])
                        stage(3.2)
                        pf, pfb = palloc(1)
                        for c in range(8):
                            S.op("pe", lambda e, c=c: e.matmul(pf[:, 0:8], lhsT=hT32[:, c, :], rhs=t_wff[:, c, :], start=(c == 0), stop=(c == 7)),
                                 reads=[b_hT32, b_wff], writes=pfb, sig=(c == 7))
                        S.op("dve", lambda e, tl=tl, pf=pf: e.tensor_tensor(out=ffs[:, tl, :], in0=pf[:, 0:8], in1=t_bf[:], op=ALU.add),
                             reads=pfb + [b_bf], writes=[b_ffs])
                        stage(3.4)
                    stage(4)
                    S.op("act", lambda e: e.activation(out=spl[:], in_=ffs[:], func=AF.Exp, scale=-1.0), reads=[b_ffs], writes=[b_spl])
                    S.op("act", lambda e: e.activation(out=spl[:], in_=spl[:], func=AF.Ln, bias=1.0, scale=1.0), reads=[b_spl], writes=[b_spl])
                    pc, pcb = palloc(1)
                    S.op("pe", lambda e: e.matmul(pc[:, 0:32], lhsT=t_tri[:], rhs=spl[:].rearrange("p a b -> p (a b)"), start=True, stop=True),
                         reads=[b_tri, b_spl], writes=pcb)
                    S.op("pe", lambda e: e.matmul(pc[:, 32:64], lhsT=t_onesf[:], rhs=spl[:].rearrange("p a b -> p (a b)"), start=True, stop=True),
                         reads=[b_onesf, b_spl], writes=pcb)
                    for tl in range(GT):
                        blk = g * GT + tl
                        c0, c1 = carry[cstate], carry[1 - cstate]
                        bc0, bc1 = b_carry[cstate], b_carry[1 - cstate]
                        S.op("dve", lambda e, blk=blk, c0=c0: e.tensor_copy(out=Off[:, :, blk], in_=c0[:]), reads=[bc0], writes=[b_Off])
                        S.op("dve", lambda e, blk=blk, c0=c0, tl=tl: e.tensor_tensor(out=Cc[:, :, blk], in0=pc[:, tl * 8:(tl + 1) * 8], in1=c0[:], op=ALU.add),
                             reads=pcb + [bc0], writes=[b_Cc])
                        S.op("dve", lambda e, c0=c0, c1=c1, tl=tl: e.tensor_tensor(out=c1[:], in0=pc[:, 32 + tl * 8:32 + (tl + 1) * 8], in1=c0[:], op=ALU.add),
                             reads=pcb + [bc0], writes=[bc1])
                        cstate = 1 - cstate
                    stage(5)
                    gs = slice(g * GN, (g + 1) * GN)
                    w0, bw0 = wload(wa1_d[:, 0, :, :], 4096)
                    for c4 in range(4):
                        proj_fm(w0, bw0, c4, hT, b_hT, GN, qF[:, c4, :], [b_qF], t_onesblk, b_onesblk, 64.0, t_gcols[:, c4:c4 + 1])
                    w1, bw1 = wload(wa1_d[:, 1, :, :], 4096)
                    for c4 in range(4):
                        proj_fm(w1, bw1, c4, hT, b_hT, GN, kF[:, c4, gs], [b_kF[g]], t_onesblk, b_onesblk, 64.0, t_gcols[:, 4 + c4:5 + c4])
                    w2, bw2 = wload(wa1_d[:, 2, :, :], 4096)
                    for c4 in range(4):
                        proj_fm(w2, bw2, c4, hT, b_hT, GN, qS[:, c4, :], [b_qS], t_onesblk, b_onesblk, 64.0, t_gcols[:, 8 + c4:9 + c4])
                    w3, bw3 = wload(wa1_d[:, 3, :, :], 4096)
                    for c4 in range(4):
                        proj_fm(w3, bw3, c4, hT, b_hT, GN, qM[:, c4, :], [b_qM], t_onesfull, b_onesfull, 128.0, t_gcols[:, 12 + c4:13 + c4])
                    w4, bw4 = wload(wa1_d[:, 4, :, :], 4096)
                    proj_fm(w4, bw4, 0, hT, b_hT, GN, kS[:, gs], [b_kS[g]], t_onesblk, b_onesblk, 64.0, t_gcols[:, 16:17])
                    for tl in range(GT):
                        blk = g * GT + tl
                        pv, pvb = palloc(1)
                        for c in range(8):
                            S.op("pe", lambda e, c=c, tl=tl, pv=pv: e.matmul(pv, lhsT=hT[:, c, tl * P:(tl + 1) * P], rhs=w4[:, c * 512 + 128:c * 512 + 256],
                                                                          start=(c == 0), stop=(c == 7)),
                                 reads=[bw4, b_hT[tl]], writes=pvb, sig=(c == 7))
                        S.op("act", lambda e, blk=blk, pv=pv: e.copy(out=vS[:, blk, :, 0:64], in_=pv.rearrange("p (h d) -> p h d", d=64)),
                             reads=pvb, writes=[b_vS[blk]])
                    w5, bw5 = wload(wa1_d[:, 5, :, :], 4096)
                    for tl in range(GT):
                        blk = g * GT + tl
                        pv, pvb = palloc(4)
                        for c in range(8):
                            S.op("pe", lambda e, c=c, tl=tl, pv=pv: e.matmul(pv, lhsT=hT[:, c, tl * P:(tl + 1) * P], rhs=w5[:, c * 512:(c + 1) * 512],
                                                                          start=(c == 0), stop=(c == 7)),
                                 reads=[bw5, b_hT[tl]], writes=pvb, sig=(c == 7))
                        S.op("act", lambda e, blk=blk, pv=pv: e.copy(out=vF[:, blk, :, 0:64], in_=pv.rearrange("p (h d) -> p h d", d=64)),
                             reads=pvb, writes=[b_vF[blk]])

                    stage(6)
                    def finish_branch(br, tl, accs, D_h, nh_per, sink):
                        k = nxt("otok")
                        for ai, (acc, accb) in enumerate(accs):
                            av = acc[:, 0:nh_per * (D_h + 8)].rearrange("p (h d) -> p h d", d=D_h + 8)
                            r = nxt("rec")
                            if sink:
                                S.op("dve", lambda e, av=av, r=r, ai=ai: e.tensor_tensor(
                                    out=rec[r][:, 0:nh_per], in0=av[:, :, D_h], in1=t_sink[:, ai * nh_per:(ai + 1) * nh_per], op=ALU.add),
                                    reads=accb + [b_sink], writes=[b_rec[r]])
                                S.op("dve", lambda e, r=r: e.reciprocal(out=rec[r][:, 0:nh_per], in_=rec[r][:, 0:nh_per]),
                                     reads=[b_rec[r]], writes=[b_rec[r]])
                            else:
                                S.op("dve", lambda e, av=av, r=r: e.reciprocal(out=rec[r][:, 0:nh_per], in_=av[:, :, D_h]),
                                     reads=accb, writes=[b_rec[r]])
                            w = nh_per * D_h
                            S.op("dve", lambda e, av=av, r=r, ai=ai, w=w: e.tensor_tensor(
                                out=otok[k][:, ai * w:(ai + 1) * w].rearrange("p (h d) -> p h d", d=D_h),
                                in0=av[:, :, 0:D_h], in1=rec[r][:, 0:nh_per].unsqueeze(2).to_broadcast([P, nh_per, D_h]), op=ALU.mult),
                                reads=accb + [b_rec[r]], writes=[b_otok[k]])
                        pt_, ptb = palloc(2)
                        ptv = pt_.bitcast(BF16)
                        for c in range(4):
                            S.op("pe", lambda e, c=c: e.transpose(out=ptv[:, c * P:(c + 1) * P], in_=otok[k][:, c * P:(c + 1) * P], identity=identb[:]),
                                 reads=[b_otok[k], b_identb], writes=ptb, sig=(c == 3))
                        S.op("act", lambda e: e.copy(out=oT[:, br * 4:(br + 1) * 4, tl * P:(tl + 1) * P],
                                                     in_=ptv.rearrange("p (c n) -> p c n", n=P)), reads=ptb, writes=[b_oT[br][tl]])

                    for tl in range(GT):
                        i = g * GT + tl
                        qc = slice(tl * P, (tl + 1) * P)
                        accs = []
                        for hg in range(2):
                            acc, accb = palloc(4)
                            accs.append((acc, accb))
                            for hh in range(4):
                                h = hg * 4 + hh
                                pr_, hf = divmod(h, 2)
                                ps_ = slice(hf * 64, hf * 64 + 64)
                                bi = nxt("bias")
                                S.op("dve", lambda e, h=h, bi=bi: e.tensor_scalar(
                                    out=biasF[bi][:, 0:i + 1], in0=Cc[:, h, 0:i + 1], scalar1=Off[:, h, i:i + 1], scalar2=None, op0=ALU.subtract),
                                    reads=[b_Cc, b_Off], writes=[b_biasF[bi]])
                                for j0 in range(0, i + 1, 4):
                                    nj = min(4, i + 1 - j0)
                                    psq, psb = palloc(3)
                                    psq = psb_full(psb)[:, 0:nj * P]
                                    for jj in range(nj):
                                        j = j0 + jj
                                        S.op("pe", lambda e, j=j, jj=jj, psq=psq: e.matmul(psq[:, jj * P:(jj + 1) * P], lhsT=kF[ps_, pr_, j * P:(j + 1) * P],
                                                                                         rhs=qF[ps_, pr_, qc], start=True, stop=True),
                                             reads=[b_kF[j // GT], b_qF], writes=psb, sig=(jj == nj - 1))
                                    ei = nxt("esw")
                                    S.op("dve", lambda e, psq=psq, ei=ei, nj=nj, j0=j0, bi=bi: e.scalar_tensor_tensor(
                                        out=esw[ei][:, 0:nj, :], in0=psq.rearrange("p (a n) -> p a n", n=P), scalar=0.125,
                                        in1=biasF[bi][:, j0:j0 + nj].unsqueeze(2).to_broadcast([P, nj, P]), op0=ALU.mult, op1=ALU.add),
                                        reads=psb + [b_biasF[bi]], writes=[b_esw[ei]])
                                    pi = ptstate["i"]; ptstate["i"] = (pi + 1) % NPT
                                    S.op("act", lambda e, ei=ei, pi=pi, nj=nj: e.activation(out=pT[pi][:, 0:nj, :], in_=esw[ei][:, 0:nj, :], func=AF.Exp),
                                         reads=[b_esw[ei]], writes=[b_pT[pi]])
                                    if j0 + nj - 1 == i:
                                        S.op("dve", lambda e, pi=pi, nj=nj: e.tensor_tensor(out=pT[pi][:, nj - 1, :], in0=pT[pi][:, nj - 1, :], in1=t_cmask[:], op=ALU.mult),
                                             reads=[b_pT[pi], b_cmask], writes=[b_pT[pi]])
                                    for jj in range(nj):
                                        j = j0 + jj
                                        S.op("pe", lambda e, j=j, jj=jj, pi=pi, hh=hh, h=h, acc=acc: e.matmul(
                                            acc[:, hh * 72:hh * 72 + 66], lhsT=pT[pi][:, jj, :], rhs=vF[:, j, h, 0:66], start=(j == 0), stop=(j == i)),
                                            reads=[b_pT[pi], b_vF[j]], writes=accb, sig=(j == i))
                        stage(7)
                        finish_branch(0, tl, accs, 64, 4, False)
                        stage(8)
                        accs = []
                        for hg in range(2):
                            acc, accb = palloc(4)
                            accs.append((acc, accb))
                            for hh in range(4):
                                h = hg * 4 + hh
                                kv = hg
                                ps_ = slice(kv * 64, kv * 64 + 64)
                                qch = hh
                                psq, psb = palloc(2)
                                nb = 2 if i > 0 else 1
                                for w_ in range(nb):
                                    j = i - (nb - 1) + w_
                                    S.op("pe", lambda e, j=j, w_=w_, psq=psq: e.matmul(
                                        psq[:, w_ * P:(w_ + 1) * P], lhsT=kS[ps_, j * P:(j + 1) * P], rhs=qS[ps_, qch, qc], start=True, stop=True),
                                        reads=[b_kS[j // GT], b_qS], writes=psb, sig=(w_ == nb - 1))
                                ei = nxt("esw")
                                S.op("act", lambda e, psq=psq, ei=ei, nb=nb: e.activation(
                                    out=esw[ei][:, 0:nb, :], in_=psq[:, 0:nb * P].rearrange("p (a n) -> p a n", n=P), func=AF.Exp, scale=0.125),
                                    reads=psb, writes=[b_esw[ei]])
                                pi = ptstate["i"]; ptstate["i"] = (pi + 1) % NPT
                                S.op("dve", lambda e, ei=ei, pi=pi, nb=nb, h=h: e.tensor_tensor(
                                    out=pT[pi][:, 0:nb, :], in0=esw[ei][:, 0:nb, :], in1=t_eb[:, h, 2 - nb:2, :], op=ALU.mult),
                                    reads=[b_esw[ei], b_eb], writes=[b_pT[pi]])
                                for w_ in range(nb):
                                    j = i - (nb - 1) + w_
                                    S.op("pe", lambda e, j=j, w_=w_, pi=pi, hh=hh, acc=acc: e.matmul(
                                        acc[:, hh * 72:hh * 72 + 66], lhsT=pT[pi][:, w_, :], rhs=vS[:, j, kv, 0:66], start=(w_ == 0), stop=(w_ == nb - 1)),
                                        reads=[b_pT[pi], b_vS[j]], writes=accb, sig=(w_ == nb - 1))
                        finish_branch(1, tl, accs, 64, 4, True)
                        stage(9)
                        accs = []
                        for hg in range(2):
                            acc, accb = palloc(4)
                            accs.append((acc, accb))
                            for hh in range(2):
                                h = hg * 2 + hh
                                psq, psb = palloc(2)
                                for mb in range(2):
                                    S.op("pe", lambda e, mb=mb, psq=psq, h=h: e.matmul(
                                        psq[:, mb * P:(mb + 1) * P], lhsT=kM[:, h, mb * P:(mb + 1) * P], rhs=qM[:, h, qc], start=True, stop=True),
                                        reads=[b_kM, b_qM], writes=psb, sig=(mb == 1))
                                pi = ptstate["i"]; ptstate["i"] = (pi + 1) % NPT
                                S.op("act", lambda e, psq=psq, pi=pi: e.activation(
                                    out=pT[pi][:, 0:2, :], in_=psq.rearrange("p (a n) -> p a n", n=P), func=AF.Exp, scale=128.0 ** -0.5),
                                    reads=psb, writes=[b_pT[pi]])
                                for mb in range(2):
                                    S.op("pe", lambda e, mb=mb, pi=pi, hh=hh, h=h, acc=acc: e.matmul(
                                        acc[:, hh * 136:hh * 136 + 130], lhsT=pT[pi][:, mb, :], rhs=vM[:, mb, h, 0:130], start=(mb == 0), stop=(mb == 1)),
                                        reads=[b_pT[pi], b_vM], writes=accb, sig=(mb == 1))
                        finish_branch(2, tl, accs, 128, 2, False)
                        stage(10)

                    stage(11)
                    for f in range(8):
                        wa, bwa = wload(wa3_d[:, f, :], 4608)
                        pps = []
                        for br in range(3):
                            pg, pgb = palloc(4)
                            for c in range(8):
                                S.op("pe", lambda e, c=c, br=br, pg=pg: e.matmul(
                                    pg, lhsT=wa[:, c * 384 + br * 128:c * 384 + (br + 1) * 128], rhs=hT[:, c, :], start=(c == 0), stop=(c == 7)),
                                    reads=[bwa] + b_hT, writes=pgb, sig=(c == 7))
                            S.op("act", lambda e, br=br, pg=pg: e.activation(out=gsb[br][:], in_=pg, func=AF.Sigmoid,
                                                                             bias=t_bgate[:, br * 8 + f:br * 8 + f + 1], scale=1.0),
                                 reads=pgb + [b_bgate], writes=[b_gsb[br]])
                            pp, ppb = palloc(4)
                            for c in range(4):
                                S.op("pe", lambda e, c=c, br=br, pp=pp: e.matmul(
                                    pp, lhsT=wa[:, 3072 + (br * 4 + c) * 128:3072 + (br * 4 + c + 1) * 128], rhs=oT[:, br * 4 + c, :],
                                    start=(c == 0), stop=(c == 3)),
                                    reads=[bwa] + b_oT[br], writes=ppb, sig=(c == 3))
                            S.op("dve", lambda e, br=br, pp=pp: e.tensor_tensor(out=tmg[br][:], in0=pp, in1=gsb[br][:], op=ALU.mult),
                                 reads=ppb + [b_gsb[br]], writes=[b_tmg[br]])
                        S.op("dve", lambda e: e.tensor_tensor(out=tmg[0][:], in0=tmg[0][:], in1=tmg[1][:], op=ALU.add),
                             reads=[b_tmg[0], b_tmg[1]], writes=[b_tmg[0]])
                        S.op("dve", lambda e, f=f: e.tensor_tensor(out=mT[:, f, :], in0=tmg[0][:], in1=tmg[2][:], op=ALU.add),
                             reads=[b_tmg[0], b_tmg[2]], writes=[b_mT[f]])
                    for tl in range(GT):
                        blk = g * GT + tl
                        r0 = tok0 + blk * P
                        xi = nxt("x1")
                        S.dma("sp", "dxr%d" % xi, lambda e, r0=r0, xi=xi: e.dma_start(out=x1t[xi][:], in_=x_d[r0:r0 + P, :]), writes=[b_x1t[xi]])
                        for hf in range(2):
                            py, pyb = palloc(4)
                            for f in range(8):
                                S.op("pe", lambda e, f=f, hf=hf, tl=tl, py=py: e.matmul(
                                    py, lhsT=mT[:, f, tl * P:(tl + 1) * P], rhs=t_wout[:, f, hf * 512:(hf + 1) * 512], start=(f == 0), stop=(f == 7)),
                                    reads=[b_mT[f], b_wout], writes=pyb, sig=(f == 7))
                            S.op("dve", lambda e, hf=hf, tl=tl, py=py, xi=xi: e.tensor_tensor(
                                out=x1t[xi][:, hf * 512:(hf + 1) * 512], in0=py, in1=x1t[xi][:, hf * 512:(hf + 1) * 512], op=ALU.add),
                                reads=pyb + [b_x1t[xi]], writes=[b_x1t[xi]])
                        S.dma("sp", "dst%d" % xi, lambda e, r0=r0, xi=xi: e.dma_start(out=out_d[r0:r0 + P, :], in_=x1t[xi][:]), reads=[b_x1t[xi]])
            stage(12)
            S.barrier()

          with contextlib.ExitStack() as sp_:
            wpq = sb(sp_, "wpq_s", [P, 8, 2048]); b_wpq = Buf("wpq")
            keysT = sb(sp_, "keys_s", [P, 16, P]); b_keys = Buf("keys")
            S.dma("sp", "dwpq", lambda e: e.dma_start(out=wpq[:], in_=wpq_d), writes=[b_wpq])
            S.dma("sp", "dwpq", lambda e: e.dma_start(out=keysT[:], in_=keys_d), writes=[b_keys])
            ring = [sb(sp_, "ring%d" % i, [P, D]) for i in range(RING)]; b_ring = [Buf("ring%d" % i) for i in range(RING)]
            rstate = {"i": 0}
            bc_reg = sp_.enter_context(nc.gpsimd.register("bcreg"))
            nc.gpsimd.reg_mov(bc_reg, nexp - 1)
            x1 = [sb(sp_, "px1_%d" % i, [P, D]) for i in range(2)]; b_x1 = [Buf("px1_0"), Buf("px1_1")]
            h2 = [sb(sp_, "ph2_%d" % i, [P, D]) for i in range(2)]; b_h2 = [Buf("ph2_0"), Buf("ph2_1")]
            acc = sb(sp_, "pacc", [P, D]); b_acc = Buf("pacc")
            g2bc = sb(sp_, "g2_s", [P, D]); b_g2 = Buf("g2")
            S.dma("sp", "dwpq", lambda e: e.dma_start(out=g2bc[:], in_=g2_d.partition_broadcast(P)), writes=[b_g2])
            junkd = sb(sp_, "junkD", [P, D]); b_junkd = Buf("junkD")
            ssp = sb(sp_, "ssP", [P, 1]); b_ssp = Buf("ssP")
            h2T = sb(sp_, "h2T", [P, 8, P]); b_h2T = Buf("h2T")
            qTs = sb(sp_, "qTs", [P, 16, P]); b_qTs = Buf("qTs")
            ssb = sb(sp_, "ssb", [P, 16, P]); b_ssb = Buf("ssb")
            rep = sb(sp_, "rep", [P, 256]); b_rep = Buf("rep")
            v16 = sb(sp_, "v16", [P, 16, 16]); b_v16 = Buf("v16")
            i16 = sb(sp_, "i16", [P, 16, 16], U32); b_i16 = Buf("i16")
            i16f = sb(sp_, "i16f", [P, 16, 16]); b_i16f = Buf("i16f")
            cand = sb(sp_, "cand", [P, 8, 16, 16]); b_cand = Buf("cand")
            svt = sb(sp_, "svt", [P, 8, 16]); b_sv = Buf("sv")
            sit = sb(sp_, "sit", [P, 8, 16], U32); b_si = Buf("si")
            ai_ = sb(sp_, "ai", [P, 8, 16], I32); b_ai = Buf("ai")
            af_ = sb(sp_, "af", [P, 8, 16]); b_af = Buf("af")
            bf_ = sb(sp_, "bff", [P, 8, 16]); b_bff = Buf("bff")
            mk = sb(sp_, "mk", [P, 8, 16, 16]); b_mk = Buf("mk")
            e1f = sb(sp_, "e1f", [P, 8, 16]); b_e1f = Buf("e1f")
            e2f = sb(sp_, "e2f", [P, 8, 16]); b_e2f = Buf("e2f")
            gsum = sb(sp_, "gsum", [P, 8]); b_gsum = Buf("gsum")
            idx = [sb(sp_, "idx%d" % i, [P, 128], I32) for i in range(2)]; b_idx = [Buf("idx0"), Buf("idx1")]
            gate = [sb(sp_, "gate%d" % i, [P, 8, 16]) for i in range(2)]; b_gate = [Buf("gate0"), Buf("gate1")]
            a_t = sb(sp_, "a_t", [P, 128]); b_a = Buf("a_t")
            act_t = sb(sp_, "act_t", [P, 128]); b_act = Buf("act_t")
            NT = NSEQ * NBLK if npeer is None else npeer

            def prep(t):
                pb_ = t % 2
                r0 = t * P
                S.dma("sp", "dpx%d" % pb_, lambda e: e.dma_start(out=x1[pb_][:], in_=out_d[r0:r0 + P, :]), writes=[b_x1[pb_]])
                rmsnorm(sp_, x1[pb_][:], b_x1[pb_], g2bc, b_g2, h2[pb_][:], b_h2[pb_], ssp, b_ssp)
                regs = transpose8(h2[pb_], b_h2[pb_])
                for hb, (pr, pbs) in enumerate(regs):
                    S.op("act", lambda e, hb=hb, pr=pr: e.copy(out=h2T[:, hb * 4:(hb + 1) * 4, :], in_=pr.rearrange("p (c n) -> p c n", n=P)),
                         reads=pbs, writes=[b_h2T])
                stage(21)
                for c4 in range(4):
                    pq, pqb = palloc(4)
                    for cc in range(4):
                        ch = c4 * 4 + cc
                        for c in range(8):
                            S.op("pe", lambda e, c=c, cc=cc, ch=ch, pq=pq: e.matmul(
                                pq[:, cc * P:(cc + 1) * P], lhsT=wpq[:, c, ch * P:(ch + 1) * P], rhs=h2T[:, c, :], start=(c == 0), stop=(c == 7)),
                                reads=[b_wpq, b_h2T], writes=pqb, sig=(c == 7 and cc == 3))
                    S.op("act", lambda e, c4=c4, pq=pq: e.copy(out=qTs[:, c4 * 4:(c4 + 1) * 4, :], in_=pq.rearrange("p (c n) -> p c n", n=P)),
                         reads=pqb, writes=[b_qTs])
                for c4 in range(4):
                    pq, pqb = palloc(4)
                    for cc in range(4):
                        ch = c4 * 4 + cc
                        S.op("pe", lambda e, cc=cc, ch=ch, pq=pq: e.matmul(
                            pq[:, cc * P:(cc + 1) * P], lhsT=qTs[:, ch, :], rhs=keysT[:, ch, :], start=True, stop=True),
                            reads=[b_qTs, b_keys], writes=pqb, sig=(cc == 3))
                    S.op("act", lambda e, c4=c4, pq=pq: e.copy(out=ssb[:, c4 * 4:(c4 + 1) * 4, :], in_=pq.rearrange("p (c n) -> p c n", n=P)),
                         reads=pqb, writes=[b_ssb])
                stage(22)
                for ch in range(16):
                    S.op("dve", lambda e, ch=ch: e.max(out=v16[:, ch, 0:8], in_=ssb[:, ch, :]), reads=[b_ssb], writes=[b_v16])
                    S.op("dve", lambda e, ch=ch: e.match_replace(out=rep[:, 0:128], in_to_replace=v16[:, ch, 0:8], in_values=ssb[:, ch, :], imm_value=-1e30),
                         reads=[b_ssb, b_v16], writes=[b_rep])
                    S.op("dve", lambda e, ch=ch: e.max(out=v16[:, ch, 8:16], in_=rep[:, 0:128]), reads=[b_rep], writes=[b_v16])
                    S.op("dve", lambda e, ch=ch: e.max_index(out=i16[:, ch, 0:8], in_max=v16[:, ch, 0:8], in_values=ssb[:, ch, :]),
                         reads=[b_ssb, b_v16], writes=[b_i16])
                    S.op("dve", lambda e, ch=ch: e.max_index(out=i16[:, ch, 8:16], in_max=v16[:, ch, 8:16], in_values=ssb[:, ch, :]),
                         reads=[b_ssb, b_v16], writes=[b_i16])
                v16v = v16[:].rearrange("p (h two) k -> p h two k", two=2)
                S.op("dve", lambda e: e.tensor_tensor(out=cand[:], in0=v16v[:, :, 0, :].unsqueeze(3).to_broadcast([P, 8, 16, 16]),
                                                      in1=v16v[:, :, 1, :].unsqueeze(2).to_broadcast([P, 8, 16, 16]), op=ALU.add),
                     reads=[b_v16], writes=[b_cand])
                for h in range(8):
                    ch_ = cand[:, h, :, :].rearrange("p a b -> p (a b)")
                    S.op("dve", lambda e, h=h, ch_=ch_: e.max(out=svt[:, h, 0:8], in_=ch_), reads=[b_cand], writes=[b_sv])
                    S.op("dve", lambda e, h=h, ch_=ch_: e.match_replace(out=rep[:], in_to_replace=svt[:, h, 0:8], in_values=ch_, imm_value=-1e30),
                         reads=[b_cand, b_sv], writes=[b_rep])
                    S.op("dve", lambda e, h=h: e.max(out=svt[:, h, 8:16], in_=rep[:]), reads=[b_rep], writes=[b_sv])
                    S.op("dve", lambda e, h=h, ch_=ch_: e.max_index(out=sit[:, h, 0:8], in_max=svt[:, h, 0:8], in_values=ch_),
                         reads=[b_cand, b_sv], writes=[b_si])
                    S.op("dve", lambda e, h=h, ch_=ch_: e.max_index(out=sit[:, h, 8:16], in_max=svt[:, h, 8:16], in_values=ch_),
                         reads=[b_cand, b_sv], writes=[b_si])
                stage(23)
                S.op("dve", lambda e: e.tensor_copy(out=i16f[:], in_=i16[:]), reads=[b_i16], writes=[b_i16f])
                S.op("dve", lambda e: e.tensor_single_scalar(out=ai_[:], in_=sit[:].bitcast(I32), scalar=4, op=ALU.logical_shift_right),
                     reads=[b_si], writes=[b_ai])
                S.op("dve", lambda e: e.tensor_copy(out=af_[:], in_=ai_[:]), reads=[b_ai], writes=[b_af])
                S.op("dve", lambda e: e.tensor_single_scalar(out=ai_[:], in_=sit[:].bitcast(I32), scalar=15, op=ALU.bitwise_and),
                     reads=[b_si, b_af], writes=[b_ai])
                S.op("dve", lambda e: e.tensor_copy(out=bf_[:], in_=ai_[:]), reads=[b_ai], writes=[b_bff])
                i16v = i16f[:].rearrange("p (h two) k -> p h two k", two=2)
                iob = iota16[:].unsqueeze(1).unsqueeze(1).to_broadcast([P, 8, 16, 16])
                for which, (src, bsrc, dst, bdst) in enumerate(((af_, b_af, e1f, b_e1f), (bf_, b_bff, e2f, b_e2f))):
                    S.op("dve", lambda e, src=src: e.tensor_tensor(out=mk[:], in0=src[:].unsqueeze(3).to_broadcast([P, 8, 16, 16]), in1=iob, op=ALU.is_equal),
                         reads=[bsrc, b_iota], writes=[b_mk])
                    S.op("dve", lambda e, which=which: e.tensor_tensor(
                        out=mk[:], in0=mk[:], in1=i16v[:, :, which, :].unsqueeze(2).to_broadcast([P, 8, 16, 16]), op=ALU.mult),
                        reads=[b_mk, b_i16f], writes=[b_mk])
                    S.op("dve", lambda e, dst=dst: e.tensor_reduce(out=dst[:], in_=mk[:], axis=AX.X, op=ALU.add), reads=[b_mk], writes=[bdst])
                S.op("dve", lambda e: e.scalar_tensor_tensor(out=e1f[:], in0=e1f[:], scalar=128.0, in1=e2f[:], op0=ALU.mult, op1=ALU.add),
                     reads=[b_e1f, b_e2f], writes=[b_e1f])
                S.op("dve", lambda e: e.tensor_copy(out=idx[pb_][:], in_=e1f[:].rearrange("p h k -> p (h k)")), reads=[b_e1f], writes=[b_idx[pb_]])
                stage(24)
                S.op("dve", lambda e: e.tensor_tensor(out=gate[pb_][:], in0=svt[:], in1=svt[:, :, 0:1].to_broadcast([P, 8, 16]), op=ALU.subtract),
                     reads=[b_sv], writes=[b_gate[pb_]])
                S.op("act", lambda e: e.activation(out=gate[pb_][:], in_=gate[pb_][:], func=AF.Exp), reads=[b_gate[pb_]], writes=[b_gate[pb_]])
                S.op("dve", lambda e: e.tensor_reduce(out=gsum[:], in_=gate[pb_][:], axis=AX.X, op=ALU.add), reads=[b_gate[pb_]], writes=[b_gsum])
                S.op("dve", lambda e: e.reciprocal(out=gsum[:], in_=gsum[:]), reads=[b_gsum], writes=[b_gsum])
                S.op("dve", lambda e: e.tensor_tensor(out=gate[pb_][:], in0=gate[pb_][:], in1=gsum[:].unsqueeze(2).to_broadcast([P, 8, 16]), op=ALU.mult),
                     reads=[b_gate[pb_], b_gsum], writes=[b_gate[pb_]])

            def gather(tab, col_ap, bidx):
                r = rstate["i"]; rstate["i"] = (r + 1) % RING
                S.dma("pool", "dg%d" % r, lambda e: e.indirect_dma_start(
                    out=ring[r][:], out_offset=None, in_=tab, in_offset=bass.IndirectOffsetOnAxis(ap=col_ap, axis=0),
                    bounds_check=bc_reg, oob_is_err=False),
                    reads=[bidx], writes=[b_ring[r]])
                return r

            def consume(t):
                pb_ = t % 2
                r0 = t * P
                stage(25)
                S.op("dve", lambda e: e.memset(a_t[:], 0.0), writes=[b_a])
                for s in range(128):
                    r = gather(u_d, idx[pb_][:, s:s + 1], b_idx[pb_])
                    S.op("dve", lambda e, s=s, r=r: e.scalar_tensor_tensor(
                        out=junkd[:], in0=ring[r][:], scalar=1.0, in1=h2[pb_][:], op0=ALU.mult, op1=ALU.mult, accum_out=a_t[:, s:s + 1]),
                        reads=[b_ring[r], b_h2[pb_]], writes=[b_junkd, b_a])
                stage(26)
                S.op("act", lambda e: e.activation(out=act_t[:], in_=a_t[:], func=AF.Gelu), reads=[b_a], writes=[b_act])
                S.op("dve", lambda e: e.tensor_tensor(out=act_t[:], in0=act_t[:], in1=gate[pb_][:].rearrange("p h k -> p (h k)"), op=ALU.mult),
                     reads=[b_act, b_gate[pb_]], writes=[b_act])
                for s in range(128):
                    r = gather(v_d, idx[pb_][:, s:s + 1], b_idx[pb_])
                    if s == 0:
                        S.op("dve", lambda e, r=r: e.tensor_scalar(out=acc[:], in0=ring[r][:], scalar1=act_t[:, 0:1], scalar2=None, op0=ALU.mult),
                             reads=[b_ring[r], b_act], writes=[b_acc])
                    else:
                        S.op("dve", lambda e, s=s, r=r: e.scalar_tensor_tensor(
                            out=acc[:], in0=ring[r][:], scalar=act_t[:, s:s + 1], in1=acc[:], op0=ALU.mult, op1=ALU.add),
                            reads=[b_ring[r], b_act, b_acc], writes=[b_acc])
                S.op("dve", lambda e: e.tensor_tensor(out=acc[:], in0=acc[:], in1=x1[pb_][:], op=ALU.add), reads=[b_acc, b_x1[pb_]], writes=[b_acc])
                S.dma("sp", "dpo", lambda e: e.dma_start(out=out_d[r0:r0 + P, :], in_=acc[:]), reads=[b_acc])

            if NT > 0:
                prep(0)
            for t in range(NT):
                if t + 1 < NT:
                    prep(t + 1)
                consume(t)
        except _StopBuild:
            pass
        S.finish()
    return nc


def _kmajor(w):
    K, N = w.shape
    return np.ascontiguousarray(w.reshape(K // P, P, N).transpose(1, 0, 2))


def _host_layout(inp):
    f32 = np.float32
    w_in = inp["w_in"][0]
    o = 0
    parts = []
    for wdt in (512, 512, 512, 8, 512, 128, 128, 512, 3072):
        parts.append(w_in[:, o:o + wdt]); o += wdt
    fq, fk, fv, ffw, sqw, skw, svw, mqw, glw = parts
    sq_perm = np.concatenate([np.concatenate([sqw[:, c * 64:(c + 1) * 64], sqw[:, (c + 4) * 64:(c + 5) * 64]], axis=1) for c in range(4)], axis=1)
    slot4 = np.zeros((D, 512), f32)
    slot4[:, 0:128] = skw
    slot4[:, 128:256] = svw
    wa1 = np.stack([_kmajor(m) for m in (fq, fk, sq_perm, mqw, slot4, fv)], axis=1)
    wmkv = inp["w_mem_kv"][0]
    wmkv_t = np.stack([_kmajor(wmkv[:, 0:512]), _kmajor(wmkv[:, 512:1024])], axis=1)
    wff = _kmajor(ffw)
    glk = _kmajor(glw)
    wo = np.stack([_kmajor(inp[k][0]) for k in ("w_fox_o", "w_swa_o", "w_mem_o")], axis=1)
    wa3 = np.zeros((P, 8, 4608), f32)
    for f in range(8):
        gpart = np.stack([glk[:, :, br * 1024 + f * 128: br * 1024 + (f + 1) * 128] for br in range(3)], axis=2)
        wa3[:, f, 0:3072] = gpart.reshape(P, 3072)
        wa3[:, f, 3072:] = wo[:, :, :, f * 128:(f + 1) * 128].reshape(P, 12 * 128)
    wout = _kmajor(inp["w_out"][0])
    wpq = _kmajor(inp["w_peer_q"][0])
    k1 = inp["peer_keys1"][0]
    k2 = inp["peer_keys2"][0]
    keysT = np.zeros((P, 16, P), f32)
    for h in range(8):
        keysT[:, 2 * h, :] = k1[h].T
        keysT[:, 2 * h + 1, :] = k2[h].T
    gcols = np.zeros((P, 18), f32)
    two = lambda g: np.concatenate([g, g])
    for c in range(4):
        gcols[:, c] = two(inp["fox_q_g"][0])
        gcols[:, 4 + c] = two(inp["fox_k_g"][0])
        gcols[:, 8 + c] = two(inp["swa_q_g"][0])
        gcols[:, 12 + c] = inp["mem_q_g"][0]
    gcols[:, 16] = two(inp["swa_k_g"][0])
    gcols[:, 17] = inp["mem_k_g"][0]
    bgate = np.ascontiguousarray(inp["b_gate"][0].reshape(24, P).T)
    kk = np.arange(P)[:, None]
    qq = np.arange(P)[None, :]
    blk = (kk // 64 == qq // 64)
    slopes = 2.0 ** (-(np.arange(8, dtype=np.float64) + 1.0))
    eb = np.zeros((P, 8, 2, P), f32)
    for h in range(8):
        eb[:, h, 0, :] = np.where(kk > qq, np.exp(-slopes[h] * (qq + P - kk)), 0.0)
        eb[:, h, 1, :] = np.where(kk <= qq, np.exp(-slopes[h] * (qq - kk)), 0.0)
    bf16 = ml_dtypes.bfloat16
    shared = dict(
        wa1=wa1, wmkv=wmkv_t, wff=wff, wa3=wa3, wout=wout, wpq=wpq, keysT=keysT,
        peer_u=np.ascontiguousarray(inp["peer_u"][0]), peer_v=np.ascontiguousarray(inp["peer_v"][0]),
        g1=inp["norm1_g"].reshape(1, D), g2=inp["norm2_g"].reshape(1, D), gm=inp["mem_norm_g"].reshape(1, D),
        bforget=inp["b_forget"].reshape(1, 8), sinks=inp["swa_sinks"].reshape(1, 8),
        gcols=gcols, bgate=bgate,
        identf=np.eye(P, dtype=f32), identb=np.eye(P, dtype=f32).astype(bf16),
        onesblk=blk.astype(f32).astype(bf16), onesfull=np.ones((P, P), f32).astype(bf16),
        tri=(kk <= qq).astype(f32), onesf=np.ones((P, P), f32),
        cmask=(kk <= qq).astype(f32).astype(bf16), eb=eb,
        iota16=np.broadcast_to(np.arange(16, dtype=f32), (P, 16)).copy(),
    )
    return {k: np.ascontiguousarray(v) for k, v in shared.items()}


def kernel(**inputs):
    inp = {k: np.asarray(v) for k, v in inputs.items()}
    shared = _host_layout(inp)
    x = np.ascontiguousarray(inp["x"], dtype=np.float32)
    mem = np.ascontiguousarray(inp["mem"], dtype=np.float32)
    in_maps = []
    for c in range(NCORES):
        m = dict(shared)
        m["x"] = x[c * NSEQ:(c + 1) * NSEQ].reshape(NSEQ * SEQ, D)
        m["mem"] = mem[c * NSEQ:(c + 1) * NSEQ].reshape(NSEQ * NMEM, D)
        in_maps.append(m)
    nc = build_program()
    res = run_bass_kernel_spmd(nc, in_maps, core_ids=list(range(NCORES)))
    outs = [np.asarray(r["out"]).reshape(NSEQ, SEQ, D) for r in res.results]
    return np.concatenate(outs, axis=0).astype(np.float32)
```
